# Optimizing a Trainium2 kernel written in Bass

```python
import math
import jax, jax.numpy as jnp
from jax import lax
import numpy as np

D_MODEL = 4096
BATCH = 4
SEQ = 4096
DEPTH = 1

CHUNK = 64
N_META = 16
Q_BLOCK = 128
NEG_INF = -1e30
RMS_EPS = 1e-6

MIX_WIDTH = D_MODEL
DIFF_WIDTH = MIX_WIDTH // 2
MLA_WIDTH = MIX_WIDTH - DIFF_WIDTH

DIFF_HEAD_DIM = 128
DIFF_V_DIM = 2 * DIFF_HEAD_DIM
DIFF_HEADS = DIFF_WIDTH // DIFF_V_DIM
DIFF_QK_WIDTH = DIFF_HEADS * 2 * DIFF_HEAD_DIM

MLA_NOPE = 128
MLA_ROPE = 64
MLA_V_DIM = 128
MLA_HEADS = MLA_WIDTH // MLA_V_DIM
Q_LORA = 1536
KV_LORA = 512
ROPE_THETA = 10000.0

IN_SIZES = (DIFF_QK_WIDTH, DIFF_QK_WIDTH, DIFF_WIDTH, DIFF_WIDTH,
            Q_LORA, KV_LORA, MLA_ROPE, MLA_WIDTH)
IN_WIDTH = sum(IN_SIZES)

kernel_name = 'hybrid_diffattn_mla_meta_chunk_causal'


def _rms_norm(x, g):
    xf = x.astype(jnp.float32)
    y = xf * lax.rsqrt(jnp.mean(xf * xf, axis=-1, keepdims=True) + RMS_EPS)
    return (y * g.astype(jnp.float32)).astype(x.dtype)


def _rope(x, cos, sin):
    half = x.shape[-1] // 2
    x1, x2 = x[..., :half], x[..., half:]
    c = cos.astype(x.dtype)
    s = sin.astype(x.dtype)
    return jnp.concatenate([x1 * c - x2 * s, x1 * s + x2 * c], axis=-1)


def _alibi_slopes(n_heads):
    return 2.0 ** (-8.0 * jnp.arange(1, n_heads + 1, dtype=jnp.float32) / n_heads)


def _chunk_id(pos):
    return jnp.where(pos < N_META, 0, (pos - N_META) // CHUNK + 1)


def _chunk_causal_mask(q_pos, k_pos):
    return _chunk_id(k_pos)[None, :] <= _chunk_id(q_pos)[:, None]


def _sweep(attend, q_arrays, pos):
    meta_out = attend(tuple(q[:, :N_META] for q in q_arrays), pos[:N_META], N_META)
    total = pos.shape[0]
    n_blk = (total - N_META) // Q_BLOCK

    def to_blocks(q):
        r = q[:, N_META:]
        r = r.reshape((r.shape[0], n_blk, Q_BLOCK) + r.shape[2:])
        return jnp.moveaxis(r, 1, 0)

    blocks = tuple(to_blocks(q) for q in q_arrays)
    pos_blocks = pos[N_META:].reshape(n_blk, Q_BLOCK)
    out = lax.map(lambda xs: attend(xs[:-1], xs[-1], total), blocks + (pos_blocks,))
    out = jnp.moveaxis(out, 0, 1)
    out = out.reshape((out.shape[0], n_blk * Q_BLOCK) + out.shape[3:])
    return jnp.concatenate([meta_out, out], axis=1)


def _diff_attention(q, k, v, lam, pos, slopes):
    scale = DIFF_HEAD_DIM ** -0.5

    def attend(qb, q_pos, n_keys):
        (qb,) = qb
        kb = k[:, :n_keys]
        vb = v[:, :n_keys]
        k_pos = pos[:n_keys]
        dist = jnp.abs(q_pos[:, None] - k_pos[None, :]).astype(jnp.float32)
        bias = -slopes[:, None, None] * dist[None]
        s = jnp.einsum('bqhmd,bkhmd->bmhqk', qb, kb,
                       preferred_element_type=jnp.float32) * scale + bias
        s = jnp.where(_chunk_causal_mask(q_pos, k_pos), s, NEG_INF)
        p = jax.nn.softmax(s, axis=-1)
        w = p[:, 0] - lam * p[:, 1]
        return jnp.einsum('bhqk,bkhe->bqhe', w.astype(vb.dtype), vb)

    return _sweep(attend, (q,), pos)


def _mla_attention(q_nope, q_rope, k_nope, k_rope, v, pos):
    scale = (MLA_NOPE + MLA_ROPE) ** -0.5

    def attend(qb, q_pos, n_keys):
        qn, qr = qb
        k_pos = pos[:n_keys]
        s = (jnp.einsum('bqhd,bkhd->bhqk', qn, k_nope[:, :n_keys], preferred_element_type=jnp.float32)
             + jnp.einsum('bqhr,bkr->bhqk', qr, k_rope[:, :n_keys], preferred_element_type=jnp.float32)) * scale
        s = jnp.where(_chunk_causal_mask(q_pos, k_pos), s, NEG_INF)
        p = jax.nn.softmax(s, axis=-1)
        vb = v[:, :n_keys]
        return jnp.einsum('bhqk,bkhe->bqhe', p.astype(vb.dtype), vb)

    return _sweep(attend, (q_nope, q_rope), pos)


def _hybrid_layer(h, pos, cos, sin, slopes, lambda_init, norm_pre, w_in,
                  lq1, lk1, lq2, lk2, subln, g_cq, g_ckv, w_uq, w_ukv, w_out, norm_post):
    b, l, _ = h.shape
    u = _rms_norm(h, norm_pre)
    z = jnp.einsum('bld,de->ble', u, w_in)
    points = []
    acc = 0
    for size in IN_SIZES[:-1]:
        acc += size
        points.append(acc)
    dq, dk, dv, dg, cq, ckv, kr, mg = jnp.split(z, points, axis=-1)

    dq = dq.reshape(b, l, DIFF_HEADS, 2, DIFF_HEAD_DIM)
    dk = dk.reshape(b, l, DIFF_HEADS, 2, DIFF_HEAD_DIM)
    dv = dv.reshape(b, l, DIFF_HEADS, DIFF_V_DIM)
    f32 = jnp.float32
    lam = (jnp.exp(jnp.sum(lq1.astype(f32) * lk1.astype(f32)))
           - jnp.exp(jnp.sum(lq2.astype(f32) * lk2.astype(f32))) + lambda_init)
    oa = _diff_attention(dq, dk, dv, lam, pos, slopes)
    oa = _rms_norm(oa, subln) * (1.0 - lambda_init)
    out_a = oa.reshape(b, l, DIFF_WIDTH) * jax.nn.silu(dg)

    cq = _rms_norm(cq, g_cq)
    q = jnp.einsum('blr,re->ble', cq, w_uq).reshape(b, l, MLA_HEADS, MLA_NOPE + MLA_ROPE)
    q_nope, q_rope = q[..., :MLA_NOPE], q[..., MLA_NOPE:]
    q_rope = _rope(q_rope, cos[:, None, :], sin[:, None, :])
    ckv = _rms_norm(ckv, g_ckv)
    kv = jnp.einsum('blr,re->ble', ckv, w_ukv).reshape(b, l, MLA_HEADS, MLA_NOPE + MLA_V_DIM)
    k_nope, v = kv[..., :MLA_NOPE], kv[..., MLA_NOPE:]
    k_rope = _rope(kr, cos, sin)
    ob = _mla_attention(q_nope, q_rope, k_nope, k_rope, v, pos)
    out_b = ob.reshape(b, l, MLA_WIDTH) * jax.nn.silu(mg)

    mix = jnp.concatenate([out_a, out_b], axis=-1)
    y = jnp.einsum('ble,ed->bld', mix, w_out)
    return h + _rms_norm(y, norm_post)


def setup_inputs(seed: int = 0) -> dict:
    key = jax.random.key(seed)
    ks = jax.random.split(key, 16)
    f32 = jnp.float32
    nrm = lambda k, shape, s: jax.random.normal(k, shape, f32) * s
    gain = lambda k, shape: 1.0 + 0.02 * jax.random.normal(k, shape, f32)
    return {
        'x': nrm(ks[0], (BATCH, SEQ, D_MODEL), 1.0),
        'meta_tokens': nrm(ks[1], (N_META, D_MODEL), 1.0),
        'norm_pre': gain(ks[2], (DEPTH, D_MODEL)),
        'w_in': nrm(ks[3], (DEPTH, D_MODEL, IN_WIDTH), D_MODEL ** -0.5),
        'diff_lambda_q1': nrm(ks[4], (DEPTH, DIFF_HEAD_DIM), 0.1),
        'diff_lambda_k1': nrm(ks[5], (DEPTH, DIFF_HEAD_DIM), 0.1),
        'diff_lambda_q2': nrm(ks[6], (DEPTH, DIFF_HEAD_DIM), 0.1),
        'diff_lambda_k2': nrm(ks[7], (DEPTH, DIFF_HEAD_DIM), 0.1),
        'diff_subln': gain(ks[8], (DEPTH, DIFF_V_DIM)),
        'mla_norm_q': gain(ks[9], (DEPTH, Q_LORA)),
        'mla_norm_kv': gain(ks[10], (DEPTH, KV_LORA)),
        'w_uq': nrm(ks[11], (DEPTH, Q_LORA, MLA_HEADS * (MLA_NOPE + MLA_ROPE)), Q_LORA ** -0.5),
        'w_ukv': nrm(ks[12], (DEPTH, KV_LORA, MLA_HEADS * (MLA_NOPE + MLA_V_DIM)), KV_LORA ** -0.5),
        'w_out': nrm(ks[13], (DEPTH, MIX_WIDTH, D_MODEL), MIX_WIDTH ** -0.5),
        'norm_post': gain(ks[14], (DEPTH, D_MODEL)),
    }


def reference(x, meta_tokens, norm_pre, w_in, diff_lambda_q1, diff_lambda_k1, diff_lambda_q2,
              diff_lambda_k2, diff_subln, mla_norm_q, mla_norm_kv, w_uq, w_ukv, w_out, norm_post):
    b = x.shape[0]
    meta = jnp.broadcast_to(meta_tokens[None].astype(x.dtype), (b, N_META, x.shape[-1]))
    h = jnp.concatenate([meta, x], axis=1)
    total = h.shape[1]
    pos = jnp.arange(total, dtype=jnp.int32)
    inv_freq = 1.0 / (ROPE_THETA ** (jnp.arange(0, MLA_ROPE, 2, dtype=jnp.float32) / MLA_ROPE))
    ang = pos.astype(jnp.float32)[:, None] * inv_freq[None, :]
    cos, sin = jnp.cos(ang), jnp.sin(ang)
    slopes = _alibi_slopes(DIFF_HEADS)
    for layer in range(DEPTH):
        lambda_init = 0.8 - 0.6 * math.exp(-0.3 * layer)
        h = _hybrid_layer(h, pos, cos, sin, slopes, lambda_init, norm_pre[layer], w_in[layer],
                          diff_lambda_q1[layer], diff_lambda_k1[layer], diff_lambda_q2[layer],
                          diff_lambda_k2[layer], diff_subln[layer], mla_norm_q[layer],
                          mla_norm_kv[layer], w_uq[layer], w_ukv[layer], w_out[layer], norm_post[layer])
    return h[:, N_META:]
```

```python
from contextlib import ExitStack
import numpy as np
import concourse.bass as bass
import concourse.mybir as mybir
from concourse.bass_utils import run_bass_kernel_spmd

F32 = mybir.dt.float32
BF16 = mybir.dt.bfloat16
AF = mybir.ActivationFunctionType
ALU = mybir.AluOpType
AX = mybir.AxisListType

ENGS = ['pe', 'act', 'dve', 'pool', 'sp']
SEM_CAP = 12000
DEBUG = False


class T:
    __slots__ = ('name', 'w', 'r')

    def __init__(self, name=''):
        self.name = name
        self.w = None
        self.r = []


class Op:
    __slots__ = ('eng', 'fn', 'raw', 'oth', 'dma', 'key', 'sig', 'sem', 'val', 'name')


class Prog:
    def __init__(self, nc):
        self.nc = nc
        self.ops = {e: [] for e in ENGS}
        self.all_dma = []
        self.last = {e: None for e in ENGS}

    def add(self, eng, fn, reads=(), writes=(), dma=0, key=None, name='', ports=()):
        op = Op()
        op.eng, op.fn, op.dma, op.key, op.sig, op.name = eng, fn, dma, key, False, name
        op.sem = None
        op.val = 0
        raw, oth = set(), set()
        for t in reads:
            if t.w is not None:
                raw.add(t.w)
        for t in writes:
            if t.w is not None:
                oth.add(t.w)
            for r in t.r:
                oth.add(r)
        for t in reads:
            if not dma:
                t.r = [r for r in t.r if r.dma or r.eng != eng]
            t.r.append(op)
        for t in writes:
            t.w = op
            t.r = []
        for t in ports:
            if t.w is not None and t.w.eng != eng:
                oth.add(t.w)
            t.w = op
        raw.discard(op)
        oth.discard(op)
        op.raw, op.oth = raw, oth
        self.ops[eng].append(op)
        if dma:
            assert key is not None
            self.all_dma.append(op)
        else:
            self.last[eng] = op
        return op

    def barrier(self):
        deps = set(self.all_dma)
        for e in ENGS:
            if self.last[e] is not None:
                deps.add(self.last[e])
        for e in ENGS:
            op = Op()
            op.eng, op.fn, op.dma, op.key, op.sig, op.name = e, None, 0, None, False, 'barrier'
            op.sem, op.val = None, 0
            op.raw = set(deps)
            op.oth = set()
            self.ops[e].append(op)
        self.all_dma = []

    def _needed(self, op):
        out = []
        for d in op.raw:
            if d.fn is None:
                continue
            if (not d.dma) and d.eng == op.eng and op.eng == 'pe' and not op.dma:
                continue
            out.append(d)
        for d in op.oth:
            if d.fn is None:
                continue
            if (not d.dma) and d.eng == op.eng and op.eng == 'pe' and not op.dma:
                continue
            out.append(d)
        return out

    def emit(self, stack):
        nc = self.nc
        need = {}
        for e in ENGS:
            for op in self.ops[e]:
                nd = self._needed(op)
                need[id(op)] = nd
                for d in nd:
                    d.sig = True
        nsem = [0]

        def newsem(tag):
            nsem[0] += 1
            return stack.enter_context(nc.semaphore(f"s{nsem[0]}_{tag}"))

        keystate = {}
        for e in ENGS:
            cur = None
            cnt = 0
            for op in self.ops[e]:
                if op.fn is None:
                    continue
                if op.dma:
                    st = keystate.get(op.key)
                    if st is None or st[1] + 16 * op.dma > SEM_CAP:
                        st = [newsem('d'), 0]
                    st[1] += 16 * op.dma
                    keystate[op.key] = st
                    op.sem, op.val = st[0], st[1]
                    op.sig = True
                elif op.sig:
                    if cur is None or cnt + 1 > SEM_CAP:
                        cur = newsem(e)
                        cnt = 0
                    cnt += 1
                    op.sem, op.val = cur, cnt
        self.nsem = nsem[0]
        handles = {'pe': 'tensor', 'act': 'scalar', 'dve': 'vector', 'pool': 'gpsimd', 'sp': 'sync'}
        with nc.Block() as block:
            for e in ENGS:
                ops = self.ops[e]

                def body(eng, ops=ops):
                    waited = {}
                    for op in ops:
                        w = {}
                        for d in need[id(op)]:
                            k = d.sem.num
                            if k not in w or w[k][1] < d.val:
                                w[k] = (d.sem, d.val)
                        for k, (s, v) in w.items():
                            if waited.get(k, 0) < v:
                                eng.wait_ge(s, v)
                                waited[k] = v
                        if op.fn is None:
                            continue
                        ins = op.fn(eng)
                        if op.dma:
                            if not isinstance(ins, (list, tuple)):
                                ins = [ins]
                            assert len(ins) == op.dma, (op.name, len(ins), op.dma)
                            for i in ins:
                                i.then_inc(op.sem, 16)
                        elif op.sig:
                            assert ins is not None, op.name
                            ins.then_inc(op.sem, 1)

                getattr(block, handles[e])(body)


class Arena:
    def __init__(self, base_ap, nbytes):
        self.base = base_ap
        self.nbytes = nbytes
        self.off = 0

    def alloc(self, free_shape, dtype, name=''):
        isz = 4 if dtype == F32 else 2
        n = int(np.prod(free_shape))
        size = (n * isz + 63) // 64 * 64
        assert self.off + size <= self.nbytes, f"SBUF arena overflow at {name}: {self.off}+{size} > {self.nbytes}"
        a = self.base[:, self.off // 4:(self.off + size) // 4]
        self.off += size
        if dtype != F32:
            a = a.bitcast(dtype)
        a = a[:, 0:n]
        if len(free_shape) == 2:
            a = a.rearrange("p (a b) -> p a b", a=free_shape[0])
        elif len(free_shape) == 3:
            a = a.rearrange("p (a b c) -> p a b c", a=free_shape[0], b=free_shape[1])
        return a


class Rot:
    def __init__(self, items):
        self.items = items
        self.i = 0

    def next(self):
        it = self.items[self.i % len(self.items)]
        self.i += 1
        return it


D = 4096
KC = 32
NTOK = 4112
NOWN = 2048
INW = 12352
C_DQ, C_DK, C_DV, C_DG, C_CQ, C_CKV, C_KR, C_MG = 0, 2048, 4096, 6144, 8192, 9728, 10240, 10304
EPS = 1e-6
NEG = -30000.0
ARENA_BYTES = 204 * 1024
SLOT_ST = {0: [1, 0, 2, 3, 5, 4, 6, 7], 1: [0, 1, 3, 2, 4, 5, 7, 6]}


def build_nc():
    nc = bass.Bass("TRN2", target_bir_lowering=False)
    dk = "ExternalOutput" if DEBUG else "Internal"

    def din(name, shape, dt=F32):
        return nc.dram_tensor(name, list(shape), dt, kind="ExternalInput").ap()

    def dscr(name, shape, dt):
        return nc.dram_tensor(name, list(shape), dt, kind=dk).ap()

    xs = din("xs", [NTOK, D])
    w_in = din("w_in", [D, INW])
    w_uq = din("w_uq", [1536, 3072])
    w_ukv = din("w_ukv", [512, 4096])
    w_out = din("w_out", [D, D])
    gpre_d = din("gpre", [128, 32])
    gcq_d = din("gcq", [128, 12])
    gckv_d = din("gckv", [128, 4])
    gsub_d = din("gsub", [128, 2])
    gpost_d = din("gpost", [1, D])
    lam4_d = din("lam4", [4, 128])
    ident_d = din("ident", [128, 128])
    cosk_d = din("cosk", [128, NTOK])
    sink_d = din("sink", [128, NTOK])
    cosq_d = din("cosq", [128, NOWN])
    sinq_d = din("sinq", [128, NOWN])
    abias_d = din("abias", [128, 8 * 33 * 8])
    ccon_d = din("ccon", [128, 32])
    mbias_d = din("mbias", [128, 36])
    b2d_d = din("b2d", [128, 9 * 128])
    out_d = nc.dram_tensor("out", [NOWN, D], F32, kind="ExternalOutput").ap()

    QD = dscr("QD", [16, 128, NOWN], BF16)
    KD = dscr("KD", [16, 128, NTOK], BF16)
    VD = dscr("VD", [NTOK, 2048], BF16)
    GD = dscr("GD", [16, 128, NOWN], F32)
    QN = dscr("QN", [16, 128, NOWN], BF16)
    QR = dscr("QR", [8, 128, NOWN], BF16)
    KN = dscr("KN", [16, 128, NTOK], BF16)
    KR = dscr("KR", [128, NTOK], BF16)
    VM = dscr("VM", [NTOK, 2048], BF16)
    GM = dscr("GM", [16, 128, NOWN], F32)
    MIX = dscr("MIX", [32, 128, NOWN], BF16)
    YS = dscr("YS", [NOWN, D], F32)
    WB = nc.dram_tensor("WB", [64, 128, 8192], BF16, kind="Internal").ap()
    WOB = nc.dram_tensor("WOB", [8, 128, 16384], BF16, kind="Internal").ap()

    stack = ExitStack()
    with stack:
        arena_t = stack.enter_context(nc.sbuf_tensor("arena", [128, ARENA_BYTES // 4], F32))
        ps_t = stack.enter_context(nc.psum_tensor("ps", [128, 4096], F32))
        P = Prog(nc)
        bank = [ps_t[:, i * 512:(i + 1) * 512] for i in range(8)]
        Tb = [T(f"bank{i}") for i in range(8)]
        dram_T = {}

        def DT(*key):
            t = dram_T.get(key)
            if t is None:
                t = T(str(key))
                dram_T[key] = t
            return t

        A = Arena(arena_t[:], ARENA_BYTES)
        identf = A.alloc([128], F32)
        identb = A.alloc([128], BF16)
        onesb = A.alloc([128], BF16)
        onesf = A.alloc([128], F32)
        gpre = A.alloc([32], F32)
        gcq = A.alloc([12], F32)
        gckv = A.alloc([4], F32)
        gsub = A.alloc([2], F32)
        lamt = A.alloc([4, 128], F32)
        lamw = A.alloc([2, 128], F32)
        lams = A.alloc([8], F32)
        Tc = T("consts")
        Tlam = T("lam")
        const_base = A.off

        ci = [0]

        def cload(dst, src):
            ci[0] += 1
            P.add('sp', lambda e: e.dma_start(out=dst, in_=src), writes=[Tc], dma=1, key=f'c{ci[0]}')

        cload(identf, ident_d)
        cload(gpre, gpre_d)
        cload(gcq, gcq_d)
        cload(gckv, gckv_d)
        cload(gsub, gsub_d)
        for i in range(4):
            cload(lamt[:, i, :], lam4_d[i:i + 1, :].partition_broadcast(128))
        P.add('dve', lambda e: e.tensor_copy(out=identb, in_=identf), reads=[Tc], writes=[Tc])
        P.add('dve', lambda e: e.memset(onesb, 1.0), writes=[Tc])
        P.add('dve', lambda e: e.memset(onesf, 1.0), writes=[Tc])
        P.add('dve', lambda e: e.memset(lams, 0.0), writes=[Tlam])
        for i in range(2):
            P.add('dve', lambda e, i=i: e.tensor_mul(out=lamw[:, i, :], in0=lamt[:, 2 * i, :], in1=lamt[:, 2 * i + 1, :]),
                  reads=[Tc], writes=[Tlam])
            P.add('dve', lambda e, i=i: e.reduce_sum(out=lams[:, i:i + 1], in_=lamw[:, i, :], axis=AX.X),
                  reads=[Tlam], writes=[Tlam])
            P.add('act', lambda e, i=i: e.activation(out=lams[:, 2 + i:3 + i], in_=lams[:, i:i + 1], func=AF.Exp),
                  reads=[Tlam], writes=[Tlam])
        P.add('dve', lambda e: e.tensor_sub(out=lams[:, 4:5], in0=lams[:, 3:4], in1=lams[:, 2:3]), reads=[Tlam], writes=[Tlam])
        P.add('dve', lambda e: e.tensor_scalar_add(out=lams[:, 5:6], in0=lams[:, 4:5], scalar1=-0.2), reads=[Tlam], writes=[Tlam])
        P.add('dve', lambda e: e.tensor_scalar_mul(out=gsub, in0=gsub, scalar1=0.8), reads=[Tc], writes=[Tc])
        neglam = lams[:, 5:6]

        def rsqrt_tile(dst, src, n, inv_n, rd_T, wr_T, tmp, tmpT):
            P.add('dve', lambda e: e.tensor_scalar(out=tmp, in0=src, scalar1=inv_n, scalar2=EPS, op0=ALU.mult, op1=ALU.add),
                  reads=rd_T, writes=[tmpT])
            P.add('act', lambda e: e.activation(out=tmp, in_=tmp, func=AF.Sqrt), reads=[tmpT], writes=[tmpT])
            P.add('dve', lambda e: e.reciprocal(out=dst, in_=tmp), reads=[tmpT], writes=wr_T)

        uT = A.alloc([32, 1040], BF16, 'uT')
        TuT = T('uT')
        xst = [(A.alloc([1024], F32), T()) for _ in range(2)]
        xbf = [(A.alloc([1024], BF16), T()) for _ in range(2)]
        junk = A.alloc([1024], BF16)
        Tjunk = T()
        ssp = A.alloc([9, 4], F32)
        Tssp = T()
        rstd_tok = A.alloc([9], F32)
        rtmp = A.alloc([9], F32)
        Trstd = T()
        Rb = [(A.alloc([128], F32), T()) for _ in range(2)]
        Rpre = A.alloc([1040], F32)
        TRpre = T()
        wsl = [(A.alloc([32, 256], BF16), T()) for _ in range(2)]
        wrot = Rot(wsl)
        cqT = A.alloc([12, 512], BF16)
        TcqT = T()
        cqn = A.alloc([12, 512], BF16)
        Tcqn = T()
        ckvT = A.alloc([4, 1040], BF16)
        TckvT = T()
        ckvn = A.alloc([4, 1040], BF16)
        Tckvn = T()
        sqb = [(A.alloc([512], BF16), T()) for _ in range(3)]
        sqrot = Rot(sqb)
        Rn = A.alloc([1040], F32)
        TRn = T()
        Rtmp = A.alloc([512], F32)
        TRtmp = T()
        stb = [(A.alloc([512], BF16), T()) for _ in range(4)]
        stbrot = Rot(stb)
        stf = [(A.alloc([512], F32), T()) for _ in range(2)]
        wrA = Rot([(A.alloc([12, 128], BF16), T()) for _ in range(2)])
        wrB = Rot([(A.alloc([12, 128], BF16), T()) for _ in range(2)])
        stfrot = Rot(stf)
        ropek = [(A.alloc([512], F32), T()) for _ in range(2)]
        ropeq = [(A.alloc([512], F32), T()) for _ in range(2)]
        rt = [(A.alloc([512], F32), T()) for _ in range(2)]
        print("stage A arena bytes", A.off)
        mm_banks = Rot([0, 1, 2, 3, 4, 5])
        dmaq = Rot(['sp'])
        stn = [0]

        def store(eng, dst, src, srcT, dstTs):
            stn[0] += 1
            P.add(eng, lambda e: e.dma_start(out=dst, in_=src), reads=[srcT], writes=dstTs, dma=1,
                  key=('st', id(srcT)))

        def load_slab(src_ap_fn, ndma, view_fn=None):
            buf, Tw = wrot.next()
            P.add('pool', lambda e: src_ap_fn(e, buf), writes=[Tw], dma=ndma, key=('w', id(Tw)))
            return buf, Tw

        slab_ids = {}
        pending_wb = []
        cur_it = [0]

        def flush_wb():
            while pending_wb:
                pending_wb.pop(0)()

        def load_slab_cached(key, fn, ndma, post_fn=None):
            flush_wb()
            idx = slab_ids.setdefault(key, len(slab_ids))
            assert idx < 64
            cache_it = idx % 2
            if cur_it[0] <= cache_it:
                buf, Tw = load_slab(fn, ndma)
                if post_fn is not None:
                    post_fn(buf, Tw)
                flat = buf.rearrange("p a b -> p (a b)")
                if cur_it[0] == cache_it:
                    pending_wb.append(lambda: P.add('sp', lambda e: e.dma_start(out=WB[idx], in_=flat), reads=[Tw], writes=[DT('WB', idx)], dma=1, key=('wbst', id(Tw))))
            else:
                buf, Tw = wrot.next()
                flat = buf.rearrange("p a b -> p (a b)")
                P.add('pool', lambda e: e.dma_start(out=flat, in_=WB[idx]), reads=[DT('WB', idx)], writes=[Tw], dma=1, key=('w', id(Tw)))
            return buf, Tw

        def win_slab(c0):
            def fn(e, buf):
                src = w_in[:, c0:c0 + 256].rearrange("(k p) n -> p k n", p=128)
                return [e.dma_start(out=buf[:, 8 * i:8 * i + 8, :], in_=src[:, 8 * i:8 * i + 8, :]) for i in range(4)]
            return load_slab_cached(('win', c0), fn, 4)

        for it in range(4):
            cur_it[0] = it
            tok0 = 16 + 1024 * it
            own0 = 512 * it
            tts = [(0, 512), (512, 512)] + ([(1024, 16)] if it == 0 else [])

            def gtok(c):
                return (c - 1024) if c >= 1024 else tok0 + c

            blocks = [(tb * 128, 128) for tb in range(8)] + ([(1024, 16)] if it == 0 else [])
            for bi, (c0, nt) in enumerate(blocks):
                g0 = gtok(c0)
                P.add('dve', lambda e, bi=bi: e.memset(ssp[:, bi, :], 0.0), writes=[Tssp])
                for xc in range(4):
                    xi = (bi * 4 + xc) % 2
                    xa, Txa = xst[xi]
                    xb_, Txb = xbf[xi]
                    P.add('sp', lambda e, xa=xa, g0=g0, nt=nt, xc=xc: e.dma_start(out=xa[0:nt, :], in_=xs[g0:g0 + nt, xc * 1024:(xc + 1) * 1024]),
                          writes=[Txa], dma=1, key=('x', xi))
                    P.add('act', lambda e, xa=xa, nt=nt, bi=bi, xc=xc: e.activation(out=junk[0:nt, :], in_=xa[0:nt, :], func=AF.Square,
                                                                                   accum_out=ssp[0:nt, bi, xc:xc + 1]),
                          reads=[Txa], writes=[Tjunk, Tssp])
                    P.add('dve', lambda e, xa=xa, xb_=xb_, nt=nt: e.tensor_copy(out=xb_[0:nt, :], in_=xa[0:nt, :]), reads=[Txa], writes=[Txb])
                    tbk = 6 + ((bi * 4 + xc) % 2)
                    pb = bank[tbk].bitcast(BF16)
                    for i in range(8):
                        P.add('pe', lambda e, i=i, xb_=xb_, nt=nt, pb=pb: e.transpose(out=pb[:, i * 128:i * 128 + nt], in_=xb_[0:nt, i * 128:(i + 1) * 128],
                                                                                   identity=identb[0:nt, 0:nt]),
                              reads=[Txb, Tc], writes=[Tb[tbk]])
                    P.add('dve', lambda e, xc=xc, c0=c0, nt=nt, pb=pb: e.tensor_tensor(
                        out=uT[:, xc * 8:(xc + 1) * 8, c0:c0 + nt], in0=pb[:, 0:1024].rearrange("p (a b) -> p a b", a=8)[:, :, 0:nt],
                        in1=gpre[:, xc * 8:(xc + 1) * 8].unsqueeze(2).to_broadcast([128, 8, nt]), op=ALU.mult),
                          reads=[Tb[tbk], Tc], writes=[TuT])
                P.add('dve', lambda e, bi=bi, nt=nt: e.reduce_sum(out=rtmp[0:nt, bi:bi + 1], in_=ssp[0:nt, bi, :], axis=AX.X), reads=[Tssp], writes=[Trstd])
                P.add('dve', lambda e, bi=bi, nt=nt: e.tensor_scalar(out=rtmp[0:nt, bi:bi + 1], in0=rtmp[0:nt, bi:bi + 1], scalar1=1.0 / D, scalar2=EPS,
                                                                     op0=ALU.mult, op1=ALU.add), reads=[Trstd], writes=[Trstd])
                P.add('act', lambda e, bi=bi, nt=nt: e.activation(out=rtmp[0:nt, bi:bi + 1], in_=rtmp[0:nt, bi:bi + 1], func=AF.Sqrt), reads=[Trstd], writes=[Trstd])
                P.add('dve', lambda e, bi=bi, nt=nt: e.reciprocal(out=rstd_tok[0:nt, bi:bi + 1], in_=rtmp[0:nt, bi:bi + 1]), reads=[Trstd], writes=[Trstd])
                rb, Trb = Rb[bi % 2]
                P.add('dve', lambda e, rb=rb, bi=bi, nt=nt: e.tensor_scalar_mul(out=rb[0:nt, :], in0=onesf[0:nt, :], scalar1=rstd_tok[0:nt, bi:bi + 1]),
                      reads=[Trstd, Tc], writes=[Trb])
                rbk = mm_banks.next()
                P.add('pe', lambda e, rb=rb, nt=nt, rbk=rbk: e.matmul(bank[rbk][:, 0:nt], lhsT=rb[0:nt, :], rhs=identf[0:nt, 0:nt], start=True, stop=True),
                      reads=[Trb, Tc], writes=[Tb[rbk]])
                P.add('act', lambda e, c0=c0, nt=nt, rbk=rbk: e.activation(out=Rpre[:, c0:c0 + nt], in_=bank[rbk][:, 0:nt], func=AF.Copy), reads=[Tb[rbk]], writes=[TRpre])

            def fm_group(wbuf, Tw, wcol, KCn, rhs, Trhs, tiles, evac):
                for (c0, n) in tiles:
                    b = mm_banks.next()
                    for kc in range(KCn):
                        P.add('pe', lambda e, b=b, kc=kc, c0=c0, n=n: e.matmul(bank[b][:, 0:n], lhsT=wbuf[:, kc, wcol:wcol + 128], rhs=rhs[:, kc, c0:c0 + n],
                                                                               start=(kc == 0), stop=(kc == KCn - 1)),
                              reads=[Tw, Trhs], writes=[Tb[b]])
                    evac(b, c0, n)

            own_tile = [(512, 512)]

            def evac_bf16_store(dst_fn):
                def ev(b, c0, n):
                    sb, Tsb = stbrot.next()
                    P.add('dve', lambda e: e.tensor_tensor(out=sb[:, 0:n], in0=bank[b][:, 0:n], in1=Rpre[:, c0:c0 + n], op=ALU.mult),
                          reads=[Tb[b], TRpre], writes=[Tsb])
                    dst, dT = dst_fn(c0, n)
                    store(dmaq.next(), dst, sb[:, 0:n], Tsb, dT)
                return ev

            def evac_gate_store(dst_fn):
                def ev(b, c0, n):
                    sf, Tsf = stfrot.next()
                    P.add('dve', lambda e: e.tensor_tensor(out=sf[:, 0:n], in0=bank[b][:, 0:n], in1=Rpre[:, c0:c0 + n], op=ALU.mult),
                          reads=[Tb[b], TRpre], writes=[Tsf])
                    P.add('act', lambda e: e.activation(out=sf[:, 0:n], in_=sf[:, 0:n], func=AF.Silu), reads=[Tsf], writes=[Tsf])
                    dst, dT = dst_fn(c0, n)
                    store(dmaq.next(), dst, sf[:, 0:n], Tsf, dT)
                return ev

            def kr_fn(e, buf):
                src = w_in[:, C_KR:C_KR + 64].rearrange("(k p) n -> p k n", p=128)
                return [e.dma_start(out=buf[:, 8 * q4:8 * q4 + 8, 0:64], in_=src[:, 8 * q4:8 * q4 + 8, :]) for q4 in range(4)]
            def kr_post(wbuf, Tw):
                for (d0, s0, wdt) in ((64, 0, 64), (128, 32, 32), (160, 0, 32), (192, 128, 64)):
                    P.add('dve', lambda e, wbuf=wbuf, d0=d0, s0=s0, wdt=wdt: e.tensor_copy(out=wbuf[:, :, d0:d0 + wdt], in_=wbuf[:, :, s0:s0 + wdt]), reads=[Tw], writes=[Tw])
            wbuf, Tw = load_slab_cached(('kr',), kr_fn, 4, kr_post)
            for (c0, n) in tts:
                g0 = gtok(c0)
                (ck, Tck), (sk, Tsk) = ropek
                P.add('sp', lambda e, ck=ck, g0=g0, n=n: e.dma_start(out=ck[:, 0:n], in_=cosk_d[:, g0:g0 + n]), writes=[Tck], dma=1, key='ck')
                P.add('sp', lambda e, sk=sk, g0=g0, n=n: e.dma_start(out=sk[:, 0:n], in_=sink_d[:, g0:g0 + n]), writes=[Tsk], dma=1, key='sk')
                P.add('dve', lambda e, ck=ck, c0=c0, n=n: e.tensor_tensor(out=ck[:, 0:n], in0=ck[:, 0:n], in1=Rpre[:, c0:c0 + n], op=ALU.mult), reads=[Tck, TRpre], writes=[Tck])
                P.add('dve', lambda e, sk=sk, c0=c0, n=n: e.tensor_tensor(out=sk[:, 0:n], in0=sk[:, 0:n], in1=Rpre[:, c0:c0 + n], op=ALU.mult), reads=[Tsk, TRpre], writes=[Tsk])
                bA = mm_banks.next()
                bB = mm_banks.next()
                for kc in range(KC):
                    P.add('pe', lambda e, kc=kc, c0=c0, n=n, bA=bA, wbuf=wbuf: e.matmul(bank[bA][:, 0:n], lhsT=wbuf[:, kc, 0:128], rhs=uT[:, kc, c0:c0 + n], start=(kc == 0), stop=(kc == KC - 1)),
                          reads=[Tw, TuT], writes=[Tb[bA]])
                for kc in range(KC):
                    P.add('pe', lambda e, kc=kc, c0=c0, n=n, bB=bB, wbuf=wbuf: e.matmul(bank[bB][:, 0:n], lhsT=wbuf[:, kc, 128:256], rhs=uT[:, kc, c0:c0 + n], start=(kc == 0), stop=(kc == KC - 1)),
                          reads=[Tw, TuT], writes=[Tb[bB]])
                (t1, Tt1), (t2, Tt2) = rt
                P.add('dve', lambda e, t1=t1, bA=bA, ck=ck, n=n: e.tensor_tensor(out=t1[:, 0:n], in0=bank[bA][:, 0:n], in1=ck[:, 0:n], op=ALU.mult), reads=[Tb[bA], Tck], writes=[Tt1])
                P.add('dve', lambda e, t2=t2, bB=bB, sk=sk, n=n: e.tensor_tensor(out=t2[:, 0:n], in0=bank[bB][:, 0:n], in1=sk[:, 0:n], op=ALU.mult), reads=[Tb[bB], Tsk], writes=[Tt2])
                sb, Tsb = stbrot.next()
                P.add('dve', lambda e, sb=sb, t1=t1, t2=t2, n=n: e.tensor_tensor(out=sb[:, 0:n], in0=t1[:, 0:n], in1=t2[:, 0:n], op=ALU.add), reads=[Tt1, Tt2], writes=[Tsb])
                store('sp', KR[:, g0:g0 + n], sb[:, 0:n], Tsb, [DT('KR', g0)])

            for sl in range(2):
                wbuf, Tw = win_slab(C_CKV + 256 * sl)
                for blk in range(2):
                    kcq = sl * 2 + blk

                    def ev(b, c0, n, kcq=kcq):
                        P.add('dve', lambda e: e.tensor_tensor(out=ckvT[:, kcq, c0:c0 + n], in0=bank[b][:, 0:n], in1=Rpre[:, c0:c0 + n], op=ALU.mult),
                              reads=[Tb[b], TRpre], writes=[TckvT])
                    fm_group(wbuf, Tw, blk * 128, KC, uT, TuT, tts, ev)
            for sl in range(0, 1):
                wbuf, Tw = win_slab(C_DK + 256 * sl)
                for blk in range(2):
                    m = sl * 2 + blk
                    fm_group(wbuf, Tw, blk * 128, KC, uT, TuT, tts,
                             evac_bf16_store(lambda c0, n, m=m: (KD[m][:, gtok(c0):gtok(c0) + n], [DT('KD', m, gtok(c0))])))
            for (c0, n) in tts:
                b = mm_banks.next()
                for kcq in range(4):
                    sq, Tsq = sqrot.next()
                    P.add('act', lambda e, sq=sq, kcq=kcq, c0=c0, n=n: e.activation(out=sq[:, 0:n], in_=ckvT[:, kcq, c0:c0 + n], func=AF.Square), reads=[TckvT], writes=[Tsq])
                    P.add('pe', lambda e, sq=sq, kcq=kcq, n=n, b=b: e.matmul(bank[b][:, 0:n], lhsT=onesb, rhs=sq[:, 0:n], start=(kcq == 0), stop=(kcq == 3)),
                          reads=[Tsq, Tc], writes=[Tb[b]])
                rsqrt_tile(Rn[:, c0:c0 + n], bank[b][:, 0:n], n, 1.0 / 512, [Tb[b]], [TRn], Rtmp[:, 0:n], TRtmp)
                for kcq in range(4):
                    P.add('dve', lambda e, kcq=kcq, c0=c0, n=n: e.scalar_tensor_tensor(out=ckvn[:, kcq, c0:c0 + n], in0=ckvT[:, kcq, c0:c0 + n], scalar=gckv[:, kcq:kcq + 1],
                                                                                       in1=Rn[:, c0:c0 + n], op0=ALU.mult, op1=ALU.mult),
                          reads=[TckvT, TRn, Tc], writes=[Tckvn])
            for sl in range(1, 8):
                wbuf, Tw = win_slab(C_DK + 256 * sl)
                for blk in range(2):
                    m = sl * 2 + blk
                    fm_group(wbuf, Tw, blk * 128, KC, uT, TuT, tts,
                             evac_bf16_store(lambda c0, n, m=m: (KD[m][:, gtok(c0):gtok(c0) + n], [DT('KD', m, gtok(c0))])))
            for half in range(2):
                def ukv_fn(e, buf, half=half):
                    v = buf.rearrange("p a b -> p (a b)").rearrange("p (k n) -> p k n", k=4)
                    src = w_ukv[:, half * 2048:(half + 1) * 2048].rearrange("(k p) n -> p k n", p=128)
                    return [e.dma_start(out=v[:, :, 1024 * i:1024 * i + 1024], in_=src[:, :, 1024 * i:1024 * i + 1024]) for i in range(2)]
                wbuf, Tw = load_slab_cached(('ukv', half), ukv_fn, 2)
                wv = wbuf.rearrange("p a b -> p (a b)").rearrange("p (k n) -> p k n", k=4)
                for hl in range(8):
                    h = half * 8 + hl

                    def ev(b, c0, n, h=h):
                        sb, Tsb = stbrot.next()
                        P.add('act', lambda e: e.activation(out=sb[:, 0:n], in_=bank[b][:, 0:n], func=AF.Copy), reads=[Tb[b]], writes=[Tsb])
                        g0 = gtok(c0)
                        store(dmaq.next(), KN[h][:, g0:g0 + n], sb[:, 0:n], Tsb, [DT('KN', h, g0)])
                    fm_group(wv, Tw, hl * 256, 4, ckvn, Tckvn, tts, ev)
                for (c0, nt) in blocks:
                    g0 = gtok(c0)
                    for hg in range(2):
                        b = mm_banks.next()
                        for i4 in range(4):
                            hl = hg * 4 + i4
                            for kcq in range(4):
                                P.add('pe', lambda e, b=b, i4=i4, hl=hl, kcq=kcq, c0=c0, nt=nt, wv=wv: e.matmul(bank[b][0:nt, i4 * 128:(i4 + 1) * 128], lhsT=ckvn[:, kcq, c0:c0 + nt],
                                                                                                           rhs=wv[:, kcq, hl * 256 + 128:hl * 256 + 256], start=(kcq == 0), stop=(kcq == 3)),
                                      reads=[Tw, Tckvn], writes=[Tb[b]])
                        sb, Tsb = stbrot.next()
                        P.add('act', lambda e, sb=sb, b=b, nt=nt: e.activation(out=sb[0:nt, :], in_=bank[b][0:nt, :], func=AF.Copy), reads=[Tb[b]], writes=[Tsb])
                        cc = half * 1024 + hg * 512
                        store(dmaq.next(), VM[g0:g0 + nt, cc:cc + 512], sb[0:nt, :], Tsb, [DT('VM', g0, cc)])

            for sl in range(8):
                wbuf, Tw = win_slab(C_DV + 256 * sl)
                for bi, (c0, nt) in enumerate(blocks):
                    g0 = gtok(c0)
                    b = mm_banks.next()
                    for kc in range(KC):
                        P.add('pe', lambda e, b=b, kc=kc, c0=c0, nt=nt, wbuf=wbuf: e.matmul(bank[b][0:nt, 0:256], lhsT=uT[:, kc, c0:c0 + nt], rhs=wbuf[:, kc, :],
                                                                                        start=(kc == 0), stop=(kc == KC - 1)),
                              reads=[Tw, TuT], writes=[Tb[b]])
                    sb, Tsb = stbrot.next()
                    P.add('act', lambda e, sb=sb, b=b, nt=nt, bi=bi: e.activation(out=sb[0:nt, 0:256], in_=bank[b][0:nt, 0:256], func=AF.Copy, scale=rstd_tok[0:nt, bi:bi + 1]),
                          reads=[Tb[b], Trstd], writes=[Tsb])
                    store(dmaq.next(), VD[g0:g0 + nt, sl * 256:(sl + 1) * 256], sb[0:nt, 0:256], Tsb, [DT('VD', g0, sl)])
            for sl in range(6):
                wbuf, Tw = win_slab(C_CQ + 256 * sl)
                for blk in range(2):
                    kcq = sl * 2 + blk

                    def ev(b, c0, n, kcq=kcq):
                        P.add('dve', lambda e: e.tensor_tensor(out=cqT[:, kcq, :], in0=bank[b][:, 0:n], in1=Rpre[:, c0:c0 + n], op=ALU.mult),
                              reads=[Tb[b], TRpre], writes=[TcqT])
                    fm_group(wbuf, Tw, blk * 128, KC, uT, TuT, own_tile, ev)
            for sl in range(0, 1):
                wbuf, Tw = win_slab(C_DQ + 256 * sl)
                for blk in range(2):
                    m = sl * 2 + blk
                    fm_group(wbuf, Tw, blk * 128, KC, uT, TuT, own_tile,
                             evac_bf16_store(lambda c0, n, m=m: (QD[m][:, own0:own0 + 512], [DT('QD', m, own0)])))
            b = mm_banks.next()
            for kcq in range(12):
                sq, Tsq = sqrot.next()
                P.add('act', lambda e, sq=sq, kcq=kcq: e.activation(out=sq, in_=cqT[:, kcq, :], func=AF.Square), reads=[TcqT], writes=[Tsq])
                P.add('pe', lambda e, sq=sq, kcq=kcq, b=b: e.matmul(bank[b], lhsT=onesb, rhs=sq, start=(kcq == 0), stop=(kcq == 11)), reads=[Tsq, Tc], writes=[Tb[b]])
            rsqrt_tile(Rn[:, 0:512], bank[b], 512, 1.0 / 1536, [Tb[b]], [TRn], Rtmp, TRtmp)
            for kcq in range(12):
                P.add('dve', lambda e, kcq=kcq: e.scalar_tensor_tensor(out=cqn[:, kcq, :], in0=cqT[:, kcq, :], scalar=gcq[:, kcq:kcq + 1], in1=Rn[:, 0:512],
                                                                       op0=ALU.mult, op1=ALU.mult), reads=[TcqT, TRn, Tc], writes=[Tcqn])
            for sl in range(1, 8):
                wbuf, Tw = win_slab(C_DQ + 256 * sl)
                for blk in range(2):
                    m = sl * 2 + blk
                    fm_group(wbuf, Tw, blk * 128, KC, uT, TuT, own_tile,
                             evac_bf16_store(lambda c0, n, m=m: (QD[m][:, own0:own0 + 512], [DT('QD', m, own0)])))
            cqn_tile = [(0, 512)]
            (cq_, Tcq_), (sq_, Tsq_) = ropeq
            P.add('sp', lambda e, cq_=cq_, own0=own0: e.dma_start(out=cq_, in_=cosq_d[:, own0:own0 + 512]), writes=[Tcq_], dma=1, key='cq')
            P.add('sp', lambda e, sq_=sq_, own0=own0: e.dma_start(out=sq_, in_=sinq_d[:, own0:own0 + 512]), writes=[Tsq_], dma=1, key='sq')
            for pair in range(8):
                def uq_fn(e, buf, pair=pair):
                    v = buf.rearrange("p a b -> p (a b)")[:, 0:4608].rearrange("p (k n) -> p k n", k=12)
                    src = w_uq[:, pair * 384:(pair + 1) * 384].rearrange("(k p) n -> p k n", p=128)
                    return [e.dma_start(out=v, in_=src)]
                wbuf, Tw = load_slab_cached(('uq', pair), uq_fn, 1)
                wv = wbuf.rearrange("p a b -> p (a b)")[:, 0:4608].rearrange("p (k n) -> p k n", k=12)
                wa, Twa = wrA.next()
                wb, Twb = wrB.next()
                for i in range(2):
                    c1 = i * 192 + 128
                    P.add('dve', lambda e, wa=wa, wv=wv, i=i, c1=c1: e.tensor_copy(out=wa[:, :, i * 64:(i + 1) * 64], in_=wv[:, :, c1:c1 + 64]), reads=[Tw], writes=[Twa])
                    P.add('dve', lambda e, wb=wb, wv=wv, i=i, c1=c1: e.tensor_copy(out=wb[:, :, i * 64:i * 64 + 32], in_=wv[:, :, c1 + 32:c1 + 64]), reads=[Tw], writes=[Twb])
                    P.add('dve', lambda e, wb=wb, wv=wv, i=i, c1=c1: e.tensor_copy(out=wb[:, :, i * 64 + 32:i * 64 + 64], in_=wv[:, :, c1:c1 + 32]), reads=[Tw], writes=[Twb])
                for i in range(2):
                    h = pair * 2 + i

                    def ev(b, c0, n, h=h):
                        sb, Tsb = stbrot.next()
                        P.add('act', lambda e: e.activation(out=sb, in_=bank[b], func=AF.Copy), reads=[Tb[b]], writes=[Tsb])
                        store(dmaq.next(), QN[h][:, own0:own0 + 512], sb, Tsb, [DT('QN', h, own0)])
                    fm_group(wv, Tw, i * 192, 12, cqn, Tcqn, cqn_tile, ev)
                bA = mm_banks.next()
                bB = mm_banks.next()
                for kcq in range(12):
                    P.add('pe', lambda e, kcq=kcq, bA=bA, wa=wa: e.matmul(bank[bA], lhsT=wa[:, kcq, :], rhs=cqn[:, kcq, :], start=(kcq == 0), stop=(kcq == 11)),
                          reads=[Twa, Tcqn], writes=[Tb[bA]])
                for kcq in range(12):
                    P.add('pe', lambda e, kcq=kcq, bB=bB, wb=wb: e.matmul(bank[bB], lhsT=wb[:, kcq, :], rhs=cqn[:, kcq, :], start=(kcq == 0), stop=(kcq == 11)),
                          reads=[Twb, Tcqn], writes=[Tb[bB]])
                (t1, Tt1), (t2, Tt2) = rt
                P.add('dve', lambda e, t1=t1, bA=bA, cq_=cq_: e.tensor_tensor(out=t1, in0=bank[bA], in1=cq_, op=ALU.mult), reads=[Tb[bA], Tcq_], writes=[Tt1])
                P.add('dve', lambda e, t2=t2, bB=bB, sq_=sq_: e.tensor_tensor(out=t2, in0=bank[bB], in1=sq_, op=ALU.mult), reads=[Tb[bB], Tsq_], writes=[Tt2])
                sb, Tsb = stbrot.next()
                P.add('dve', lambda e, sb=sb, t1=t1, t2=t2: e.tensor_tensor(out=sb, in0=t1, in1=t2, op=ALU.add), reads=[Tt1, Tt2], writes=[Tsb])
                store(dmaq.next(), QR[pair][:, own0:own0 + 512], sb, Tsb, [DT('QR', pair, own0)])
            for sl in range(8):
                wbuf, Tw = win_slab(C_DG + 256 * sl)
                for blk in range(2):
                    m = sl * 2 + blk
                    fm_group(wbuf, Tw, blk * 128, KC, uT, TuT, own_tile,
                             evac_gate_store(lambda c0, n, m=m: (GD[m][:, own0:own0 + 512], [DT('GD', m, own0)])))
            for sl in range(8):
                wbuf, Tw = win_slab(C_MG + 256 * sl)
                for blk in range(2):
                    m = sl * 2 + blk
                    fm_group(wbuf, Tw, blk * 128, KC, uT, TuT, own_tile,
                             evac_gate_store(lambda c0, n, m=m: (GM[m][:, own0:own0 + 512], [DT('GM', m, own0)])))
            flush_wb()

        P.barrier()
        if STAGES >= 2:
            stage_b(nc, P, A, const_base, bank, Tb, dict(
                identb=identb, onesb=onesb, Tc=Tc, neglam=neglam, Tlam=Tlam, gsub=gsub,
                abias_d=abias_d, mbias_d=mbias_d, b2d_d=b2d_d, ccon_d=ccon_d,
                QD=QD, KD=KD, VD=VD, GD=GD, QN=QN, QR=QR, KN=KN, KR=KR, VM=VM, GM=GM, MIX=MIX, w_out=w_out, WOB=WOB))
            P.barrier()
        if STAGES >= 3:
            stage_c(nc, P, A, const_base, bank, Tb, dict(MIX=MIX, YS=YS, w_out=w_out, xs=xs, gpost_d=gpost_d, out_d=out_d, WOB=WOB))
        else:
            A.off = const_base
            z = A.alloc([4096], F32)
            Tz = T()
            P.add('dve', lambda e: e.memset(z, 0.0), writes=[Tz])
            To = T()
            for i in range(16):
                P.add('sp', lambda e, i=i: e.dma_start(out=out_d[i * 128:(i + 1) * 128, :], in_=z), reads=[Tz], writes=[To], dma=1, key='oz')
            P.barrier()
        P.emit(stack)
        print("semaphores used:", P.nsem, "ops:", {e: len(P.ops[e]) for e in ENGS})
    return nc


STAGES = 3


def stage_b(nc, P, A, const_base, bank, Tb, c):
    identb, onesb, Tc, neglam, Tlam, gsub = c['identb'], c['onesb'], c['Tc'], c['neglam'], c['Tlam'], c['gsub']
    QD, KD, VD, GD, QN, QR, KN, KR, VM, GM, MIX = (c[k] for k in ['QD', 'KD', 'VD', 'GD', 'QN', 'QR', 'KN', 'KR', 'VM', 'GM', 'MIX'])
    A.off = const_base
    abias = A.alloc([8 * 33 * 8], F32)
    mbias = A.alloc([36], F32)
    ccon = A.alloc([32], F32)
    b2d = A.alloc([9 * 128], F32)
    KRT = A.alloc([NTOK], BF16)
    KRT2 = A.alloc([NTOK], BF16)
    epsc = A.alloc([1], F32)
    Tt = T('tables')
    P.add('sp', lambda e: e.dma_start(out=abias, in_=c['abias_d']), writes=[Tt], dma=1, key='tb1')
    P.add('sp', lambda e: e.dma_start(out=mbias, in_=c['mbias_d']), writes=[Tt], dma=1, key='tb2')
    P.add('sp', lambda e: e.dma_start(out=b2d, in_=c['b2d_d']), writes=[Tt], dma=1, key='tb3')
    P.add('sp', lambda e: e.dma_start(out=KRT, in_=KR), writes=[Tt], dma=1, key='tb4')
    P.add('sp', lambda e: e.dma_start(out=KRT2, in_=KR), writes=[Tt], dma=1, key='tb6')
    P.add('dve', lambda e: e.memset(KRT[64:128, :], 0.0), reads=[Tt], writes=[Tt])
    P.add('dve', lambda e: e.memset(KRT2[0:64, :], 0.0), reads=[Tt], writes=[Tt])
    P.add('dve', lambda e: e.memset(epsc, EPS), writes=[Tt])
    P.add('sp', lambda e: e.dma_start(out=ccon, in_=c['ccon_d']), writes=[Tt], dma=1, key='tb5')
    KTb = Rot([(A.alloc([NTOK], BF16), T()) for _ in range(4)])
    Vb = Rot([(A.alloc([32, 256], BF16), A.alloc([256], BF16), T()) for _ in range(2)])
    QTb = Rot([(A.alloc([NOWN], BF16), T()) for _ in range(4)])
    QRb = Rot([(A.alloc([NOWN], BF16), T()) for _ in range(2)])
    PT = Rot([(A.alloc([512], BF16), [T() for _ in range(4)]) for _ in range(4)])
    tmp2 = Rot([(A.alloc([128], F32), T()) for _ in range(2)])
    On = [(A.alloc([2, 512], F32), T()) for _ in range(4)]
    rden = Rot([(A.alloc([512], F32), T()) for _ in range(2)])
    gate = Rot([(A.alloc([512], F32), T()) for _ in range(4)])
    sqo = A.alloc([2, 512], BF16)
    Tsqo = T()
    rs = A.alloc([512], F32)
    Trs = T()
    rstmp = A.alloc([512], F32)
    Trstmp = T()
    mst = Rot([(A.alloc([512], BF16), T()) for _ in range(3)])
    wost = A.alloc([32, 512], BF16)
    Twost = T()
    print("stage B arena bytes", A.off)
    Sb = Rot([0, 1, 2])
    Tport = {0: T('port0'), 1: T('port1'), 2: T('port2')}
    Ob = Rot([(3, 4), (5, 6)])
    Db = Rot([7])
    dsb = Rot([(A.alloc([512], F32), T()) for _ in range(2)])

    def load_kt(src):
        buf, Tk = KTb.next()
        P.add('pool', lambda e: e.dma_start(out=buf, in_=src), writes=[Tk], dma=1, key=('kt', id(Tk)))
        return buf, Tk

    def load_qt(src, rot):
        buf, Tq = rot.next()
        P.add('pool', lambda e: e.dma_start(out=buf, in_=src), writes=[Tq], dma=1, key=('qt', id(Tq)))
        return buf, Tq

    def load_v(src, col0, E):
        vb, vm, Tv = Vb.next()
        s3 = src[16:NTOK, col0:col0 + E].rearrange("(n p) e -> p n e", p=128)

        def fn(e):
            ins = [e.dma_start(out=vb[:, 8 * i:8 * i + 8, 0:E], in_=s3[:, 8 * i:8 * i + 8, :]) for i in range(4)]
            ins.append(e.dma_start(out=vm[0:16, 0:E], in_=src[0:16, col0:col0 + E]))
            return ins
        P.add('pool', fn, writes=[Tv], dma=5, key=('v', id(Tv)))
        return vb, vm, Tv

    tiles = []

    def make_unit(src, E, hb, scale, j, epilogue, pre=None):
        neh = E // 128
        ust = {}
        seq = [('meta', 0, 0)] + [('full', s, b) for s in range(2 * j + 1) for b in range(4)] + [('diag', 2 * j + 1, b) for b in range(4)]
        for idx, (cls, s, b) in enumerate(seq):
            first = idx == 0
            last = idx == len(seq) - 1
            if cls == 'meta':
                k0, nk, q0, ktile = 0, 16, 0, 0
            else:
                k0, nk, ktile = 16 + 512 * s + 128 * b, 128, 1 + 4 * s + b
                q0 = 128 * b if cls == 'diag' else 0
            st = {}

            def qk(st=st, k0=k0, nk=nk, q0=q0, first=first):
                if first:
                    ust['ob'] = Ob.next()
                    ust['db'] = Db.next()
                KT, Tk, QT, Tq = src['KT'], src['Tk'], src['QT'], src['Tq']
                rope = src.get('rope')
                sbk = Sb.next()
                st['sbk'] = sbk
                S = bank[sbk]
                P.add('pe', lambda e: e.matmul(S[0:nk, q0:512], lhsT=KT[:, k0:k0 + nk], rhs=QT[:, 512 * j + q0:512 * j + 512], start=True, stop=(rope is None)),
                      reads=[Tk, Tq], writes=[Tb[sbk]])
                if rope is not None:
                    QRT, Tqr, pr = rope
                    KRx = KRT if pr == 0 else KRT2
                    P.add('pe', lambda e: e.matmul(S[0:nk, q0:512], lhsT=KRx[:, k0:k0 + nk], rhs=QRT[:, 512 * j + q0:512 * j + 512], start=False, stop=True),
                          reads=[Tt, Tqr], writes=[Tb[sbk]])

            def ex(st=st, cls=cls, s=s, b=b, nk=nk, q0=q0, ktile=ktile):
                sbk = st['sbk']
                S = bank[sbk]
                pt, Tpts = PT.next()
                st['pt'], st['Tpts'] = pt, Tpts
                a0 = q0 // 128
                a_lo = a0

                def seg_T(lo, hi):
                    return [Tpts[a] for a in range(lo // 128, (hi + 127) // 128)]
                if cls == 'diag':
                    t2, Tt2 = tmp2.next()
                    P.add('dve', lambda e: e.scalar_tensor_tensor(out=t2, in0=S[:, a0 * 128:(a0 + 1) * 128], scalar=scale, in1=b2d[:, hb * 128:(hb + 1) * 128], op0=ALU.mult, op1=ALU.add),
                          reads=[Tb[sbk], Tt], writes=[Tt2], ports=[Tport[sbk]])
                    if hb == 8:
                        P.add('act', lambda e: e.activation(out=pt[:, a0 * 128:(a0 + 1) * 128], in_=t2, func=AF.Exp), reads=[Tt2], writes=[Tpts[a0]])
                    else:
                        cc = a0 * 8 + hb
                        P.add('act', lambda e: e.activation(out=pt[:, a0 * 128:(a0 + 1) * 128], in_=t2, func=AF.Exp, bias=ccon[:, cc:cc + 1]), reads=[Tt2, Tt], writes=[Tpts[a0]])
                    a_lo = a0 + 1
                lo = a_lo * 128
                if lo < 512:
                    if hb == 8:
                        col = j * 9 + (0 if cls == 'meta' else 1 + s)
                        segs = [(lo, 512, mbias[0:nk, col:col + 1])]
                    elif hb < 2:
                        segs = []
                        for hf in range(2):
                            l2, h2 = max(lo, 256 * hf), 256 * (hf + 1)
                            if l2 < h2:
                                col = ((2 * j + hf) * 33 + ktile) * 8 + hb
                                segs.append((l2, h2, abias[0:nk, col:col + 1]))
                    else:
                        col = ((2 * j) * 33 + ktile) * 8 + hb
                        segs = [(lo, 512, abias[0:nk, col:col + 1])]
                    for (l2, h2, bias_ap) in segs:
                        P.add('act', lambda e, l2=l2, h2=h2, bias_ap=bias_ap: e.activation(out=pt[0:nk, l2:h2], in_=S[0:nk, l2:h2], func=AF.Exp, bias=bias_ap, scale=scale),
                              reads=[Tb[sbk], Tt], writes=seg_T(l2, h2), ports=[Tport[sbk]])

            def pv(st=st, cls=cls, s=s, b=b, nk=nk, q0=q0, first=first, last=last):
                pt, Tpts = st['pt'], st['Tpts']
                vb, vm, Tv = src['vb'], src['vm'], src['Tv']
                ob, db = ust['ob'], ust['db']
                rT = [Tpts[a] for a in range(q0 // 128, 4)]
                for eh in range(neh):
                    lhsT = vm[0:16, eh * 128:(eh + 1) * 128] if cls == 'meta' else vb[:, 4 * s + b, eh * 128:(eh + 1) * 128]
                    P.add('pe', lambda e, eh=eh, lhsT=lhsT: e.matmul(bank[ob[eh]][:, q0:512], lhsT=lhsT, rhs=pt[0:nk, q0:512], start=first, stop=last),
                          reads=[Tv] + rT, writes=[Tb[ob[eh]]])
                P.add('pe', lambda e: e.matmul(bank[db][:, q0:512], lhsT=onesb[0:nk, :], rhs=pt[0:nk, q0:512], start=first, stop=last),
                      reads=[Tc] + rT, writes=[Tb[db]])

            tiles.append(dict(qk=qk, ex=ex, pv=pv, post=None, pre=(pre if first else None)))
        tiles[-1]['post'] = lambda: epilogue(ust['ob'], ust['db'])

    def normalize(ob, db, neh, dst, Tdst):
        rd, Trd = rden.next()
        ds, Tds = dsb.next()
        P.add('act', lambda e: e.activation(out=ds, in_=bank[db], func=AF.Copy), reads=[Tb[db]], writes=[Tds])
        P.add('dve', lambda e: e.reciprocal(out=rd, in_=ds), reads=[Tds], writes=[Trd])
        for eh in range(neh):
            P.add('dve', lambda e, eh=eh: e.tensor_tensor(out=dst[:, eh, :], in0=bank[ob[eh]], in1=rd, op=ALU.mult), reads=[Tb[ob[eh]], Trd], writes=[Tdst])

    sc_d = 128.0 ** -0.5
    sc_m = 192.0 ** -0.5
    oni = [0]
    for h in range(8):
        src0, src1 = {}, {}

        def pre_d(h=h, src0=src0, src1=src1):
            vb, vm, Tv = load_v(VD, h * 256, 256)
            for m, sr in ((0, src0), (1, src1)):
                sr['vb'], sr['vm'], sr['Tv'] = vb, vm, Tv
                sr['KT'], sr['Tk'] = load_kt(KD[2 * h + m])
                sr['QT'], sr['Tq'] = load_qt(QD[2 * h + m], QTb)
        for j in range(4):
            o0, To0 = On[oni[0] % 4]
            o1, To1 = On[(oni[0] + 1) % 4]
            oni[0] += 2

            def epi0(ob, db, o0=o0, To0=To0):
                normalize(ob, db, 2, o0, To0)
                return None

            def epi1(ob, db, o0=o0, To0=To0, o1=o1, To1=To1, h=h, j=j):
                normalize(ob, db, 2, o1, To1)
                for eh in range(2):
                    P.add('dve', lambda e, eh=eh: e.scalar_tensor_tensor(out=o0[:, eh, :], in0=o1[:, eh, :], scalar=neglam, in1=o0[:, eh, :], op0=ALU.mult, op1=ALU.add),
                          reads=[To0, To1, Tlam], writes=[To0])
                    P.add('dve', lambda e, eh=eh: e.tensor_tensor(out=sqo[:, eh, :], in0=o0[:, eh, :], in1=o0[:, eh, :], op=ALU.mult), reads=[To0], writes=[Tsqo])

                def part2():
                    sbk = Sb.next()
                    Sb.next()
                    Sb.next()
                    for eh in range(2):
                        P.add('pe', lambda e, eh=eh: e.matmul(bank[sbk], lhsT=onesb, rhs=sqo[:, eh, :], start=(eh == 0), stop=(eh == 1)), reads=[Tsqo, Tc], writes=[Tb[sbk]])
                    P.add('act', lambda e: e.activation(out=rstmp, in_=bank[sbk], func=AF.Ln, bias=epsc[:, 0:1], scale=1.0 / 256), reads=[Tb[sbk], Tt], writes=[Trstmp], ports=[Tport[sbk]])
                    P.add('act', lambda e: e.activation(out=rs, in_=rstmp, func=AF.Exp, scale=-0.5), reads=[Trstmp], writes=[Trs])
                    for eh in range(2):
                        g, Tg = gate.next()
                        ch = 2 * h + eh
                        P.add('sp', lambda e, g=g, ch=ch: e.dma_start(out=g, in_=GD[ch][:, 512 * j:512 * j + 512]), writes=[Tg], dma=1, key=('g', id(Tg)))
                        P.add('dve', lambda e, eh=eh: e.tensor_tensor(out=o0[:, eh, :], in0=o0[:, eh, :], in1=rs, op=ALU.mult), reads=[To0, Trs], writes=[To0])
                        ms, Tms = mst.next()
                        P.add('dve', lambda e, eh=eh, g=g, ms=ms: e.scalar_tensor_tensor(out=ms, in0=o0[:, eh, :], scalar=gsub[:, eh:eh + 1], in1=g, op0=ALU.mult, op1=ALU.mult),
                              reads=[To0, Tg, Tc], writes=[Tms])
                        P.add('sp', lambda e, ms=ms, ch=ch: e.dma_start(out=MIX[ch][:, 512 * j:512 * j + 512], in_=ms), reads=[Tms], writes=[T()], dma=1, key=('ms', id(Tms)))
                return part2
            make_unit(src0, 256, h, sc_d, j, epi0, pre=(pre_d if j == 0 else None))
            make_unit(src1, 256, h, sc_d, j, epi1)
    qr_state = {}
    for h in range(16):
        srcm = {}

        def pre_m(h=h, srcm=srcm):
            srcm['vb'], srcm['vm'], srcm['Tv'] = load_v(VM, h * 128, 128)
            srcm['KT'], srcm['Tk'] = load_kt(KN[h])
            srcm['QT'], srcm['Tq'] = load_qt(QN[h], QTb)
            if h % 2 == 0:
                qr_state['QRT'], qr_state['Tqr'] = load_qt(QR[h // 2], QRb)
            srcm['rope'] = (qr_state['QRT'], qr_state['Tqr'], 64 * (h % 2))
            if h >= 1 and h <= 8:
                ns = h - 1
                w_out, WOB = c['w_out'], c['WOB']

                def wfn(e, ns=ns):
                    srcw = w_out[:, ns * 512:(ns + 1) * 512].rearrange("(k p) n -> p k n", p=128)
                    return [e.dma_start(out=wost[:, 8 * i:8 * i + 8, :], in_=srcw[:, 8 * i:8 * i + 8, :]) for i in range(4)]
                P.add('pool', wfn, writes=[Twost], dma=4, key='wost')
                P.add('pool', lambda e, ns=ns: e.dma_start(out=WOB[ns], in_=wost.rearrange("p a b -> p (a b)")), reads=[Twost], writes=[T()], dma=1, key='wost2')
        for j in range(4):
            o0, To0 = On[oni[0] % 4]
            oni[0] += 1

            def epim(ob, db, o0=o0, To0=To0, h=h, j=j):
                normalize(ob, db, 1, o0, To0)
                g, Tg = gate.next()
                P.add('sp', lambda e: e.dma_start(out=g, in_=GM[h][:, 512 * j:512 * j + 512]), writes=[Tg], dma=1, key=('g', id(Tg)))
                ms, Tms = mst.next()
                P.add('dve', lambda e: e.tensor_tensor(out=ms, in0=o0[:, 0, :], in1=g, op=ALU.mult), reads=[To0, Tg], writes=[Tms])
                P.add('sp', lambda e: e.dma_start(out=MIX[16 + h][:, 512 * j:512 * j + 512], in_=ms), reads=[Tms], writes=[T()], dma=1, key=('ms', id(Tms)))
                return None
            make_unit(srcm, 128, 8, sc_m, j, epim, pre=(pre_m if j == 0 else None))

    deferred = []
    n = len(tiles)

    def start(t):
        if t['pre'] is not None:
            t['pre']()
        t['qk']()
    LOOK = 2
    for i in range(min(LOOK, n)):
        start(tiles[i])
    for i, t in enumerate(tiles):
        if i + LOOK < n:
            start(tiles[i + LOOK])
        t['ex']()
        t['pv']()
        while deferred and deferred[0][0] <= i:
            deferred.pop(0)[1]()
        if t['post'] is not None:
            d = t['post']()
            if d is not None:
                deferred.append((i + 10, d))
    for _, d in deferred:
        d()


def stage_c(nc, P, A, const_base, bank, Tb, c):
    MIX, YS, w_out, xs, gpost_d, out_d = c['MIX'], c['YS'], c['w_out'], c['xs'], c['gpost_d'], c['out_d']
    WOB = c['WOB']
    A.off = const_base
    mixT = A.alloc([32, 1024], BF16)
    Tmix = T()
    wsl = [(A.alloc([32, 512], BF16), T()) for _ in range(2)]
    yst = Rot([(A.alloc([512], F32), T()) for _ in range(3)])
    junk = A.alloc([512], BF16)
    Tjunk = T()
    ssy = [A.alloc([8, 8], F32) for _ in range(2)]
    Tssy = [T(), T()]
    rsy = [A.alloc([8], F32) for _ in range(2)]
    rsyt = [A.alloc([8], F32) for _ in range(2)]
    Trsy = [T(), T()]
    gpost = A.alloc([D], F32)
    Tgp = T()
    yrow = [(A.alloc([1024], F32), T()) for _ in range(4)]
    xrow = [(A.alloc([1024], F32), T()) for _ in range(4)]
    print("stage C arena bytes", A.off)
    P.add('sp', lambda e: e.dma_start(out=gpost, in_=gpost_d.partition_broadcast(128)), writes=[Tgp], dma=1, key='gp')
    mmb = Rot([0, 1, 2, 3, 4, 5, 6, 7])
    mix3 = MIX.rearrange("c p t -> p c t")
    Tout = T()
    TYS = {}

    def load_w(g):
        wbuf, Tw = wsl[g % 2]
        ns = g % 8
        P.add('sp', lambda e: e.dma_start(out=wbuf.rearrange("p a b -> p (a b)"), in_=WOB[ns]), writes=[Tw], dma=1, key=('wo', id(Tw)))

    def final_pass(half, tb):
        P.add('dve', lambda e: e.reduce_sum(out=rsyt[half][:, tb:tb + 1], in_=ssy[half][:, tb, :], axis=AX.X), reads=[Tssy[half]], writes=[Trsy[half]])
        P.add('dve', lambda e: e.tensor_scalar(out=rsyt[half][:, tb:tb + 1], in0=rsyt[half][:, tb:tb + 1], scalar1=1.0 / D, scalar2=EPS, op0=ALU.mult, op1=ALU.add),
              reads=[Trsy[half]], writes=[Trsy[half]])
        P.add('act', lambda e: e.activation(out=rsyt[half][:, tb:tb + 1], in_=rsyt[half][:, tb:tb + 1], func=AF.Sqrt), reads=[Trsy[half]], writes=[Trsy[half]])
        P.add('dve', lambda e: e.reciprocal(out=rsy[half][:, tb:tb + 1], in_=rsyt[half][:, tb:tb + 1]), reads=[Trsy[half]], writes=[Trsy[half]])
        r0 = half * 1024 + tb * 128
        jq = r0 // 512
        g0 = 16 + 512 * (2 * jq + 1) + (r0 % 512)
        for q in range(4):
            yr, Tyr = yrow[q]
            xr, Txr = xrow[q]
            P.add('pool', lambda e, yr=yr, q=q: e.dma_start(out=yr, in_=YS[r0:r0 + 128, q * 1024:(q + 1) * 1024]),
                  reads=[TYS[(half, tb, ns)] for ns in range(q * 2, q * 2 + 2)], writes=[Tyr], dma=1, key=('yr', id(Tyr)))
            P.add('act', lambda e, xr=xr, q=q: e.dma_start(out=xr, in_=xs[g0:g0 + 128, q * 1024:(q + 1) * 1024]), writes=[Txr], dma=1, key=('xr', id(Txr)))
        for q in range(4):
            yr, Tyr = yrow[q]
            xr, Txr = xrow[q]
            P.add('dve', lambda e, yr=yr, q=q: e.scalar_tensor_tensor(out=yr, in0=yr, scalar=rsy[half][:, tb:tb + 1], in1=gpost[:, q * 1024:(q + 1) * 1024], op0=ALU.mult, op1=ALU.mult),
                  reads=[Tyr, Trsy[half], Tgp], writes=[Tyr])
            P.add('dve', lambda e, yr=yr, xr=xr: e.tensor_tensor(out=yr, in0=yr, in1=xr, op=ALU.add), reads=[Tyr, Txr], writes=[Tyr])
            P.add('sp', lambda e, yr=yr, q=q: e.dma_start(out=out_d[r0:r0 + 128, q * 1024:(q + 1) * 1024], in_=yr), reads=[Tyr], writes=[Tout], dma=1, key=('yo', id(Tyr)))

    load_w(0)
    for half in range(2):
        def mfn(e, half=half):
            return [e.dma_start(out=mixT[:, 8 * i:8 * i + 8, :], in_=mix3[:, 8 * i:8 * i + 8, half * 1024:(half + 1) * 1024]) for i in range(4)]
        P.add('pool', mfn, writes=[Tmix], dma=4, key='mix')
        P.add('dve', lambda e, half=half: e.memset(ssy[half], 0.0), writes=[Tssy[half]])
        for ns in range(8):
            g = half * 8 + ns
            wbuf, Tw = wsl[g % 2]
            if g + 1 < 16:
                load_w(g + 1)
            for tb in range(8):
                b = mmb.next()
                for ec in range(32):
                    P.add('pe', lambda e, b=b, ec=ec, tb=tb, wbuf=wbuf: e.matmul(bank[b], lhsT=mixT[:, ec, tb * 128:(tb + 1) * 128], rhs=wbuf[:, ec, :], start=(ec == 0), stop=(ec == 31)),
                          reads=[Tmix, Tw], writes=[Tb[b]])
                ys, Tys = yst.next()
                P.add('dve', lambda e, b=b, ys=ys: e.tensor_copy(out=ys, in_=bank[b]), reads=[Tb[b]], writes=[Tys])
                P.add('act', lambda e, ys=ys, tb=tb, ns=ns, half=half: e.activation(out=junk, in_=ys, func=AF.Square, accum_out=ssy[half][:, tb, ns:ns + 1]), reads=[Tys], writes=[Tjunk, Tssy[half]])
                r0 = half * 1024 + tb * 128
                TYS[(half, tb, ns)] = T()
                P.add('sp', lambda e, ys=ys, r0=r0, ns=ns: e.dma_start(out=YS[r0:r0 + 128, ns * 512:(ns + 1) * 512], in_=ys), reads=[Tys], writes=[TYS[(half, tb, ns)]], dma=1, key=('ys', id(Tys)))
            if half == 1:
                final_pass(0, ns)
    for tb in range(8):
        final_pass(1, tb)
    P.barrier()


def _tables(par):
    st = SLOT_ST[par]
    pos = np.zeros(NTOK, np.int64)
    pos[:16] = np.arange(16) - 16
    for s in range(8):
        pos[16 + 512 * s:16 + 512 * (s + 1)] = 512 * st[s] + np.arange(512)
    inv_freq = (1.0 / (10000.0 ** (np.arange(0, 64, 2, dtype=np.float32) / 64.0))).astype(np.float32)
    ang = (pos + 16).astype(np.float32)[:, None] * inv_freq[None, :]
    cos = np.cos(ang).astype(np.float32)
    sin = np.sin(ang).astype(np.float32)
    p = np.arange(128)
    sign = np.where((p % 64) < 32, -1.0, 1.0).astype(np.float32)
    cosk = np.ascontiguousarray(cos[:, p % 32].T)
    sink = np.ascontiguousarray(sin[:, p % 32].T * sign[:, None])
    own = np.concatenate([np.arange(16 + 512 * (2 * j + 1), 16 + 512 * (2 * j + 2)) for j in range(4)])
    cosq = np.ascontiguousarray(cosk[:, own])
    sinq = np.ascontiguousarray(sink[:, own])
    slopes = (2.0 ** (-(np.arange(1, 9, dtype=np.float32)))).astype(np.float32)
    abias = np.zeros((128, 8, 33, 8), np.float32)
    kl = np.arange(128)
    ccon = np.zeros((128, 4, 8), np.float32)
    for hb in range(8):
        for a in range(4):
            ref_off = (256 * (a // 2) + 128) if hb < 2 else 256
            ccon[:, a, hb] = slopes[hb] * (128 * a + 64 - ref_off)
    for j in range(4):
        qst = st[2 * j + 1]
        for hf in range(2):
            g = 2 * j + hf
            for hb in range(8):
                qref = 512 * qst + ((256 * hf + 128) if hb < 2 else 256)
                kp = np.where(kl < 16, kl - 16, -16)
                abias[:, g, 0, hb] = (kp - qref) * slopes[hb]
                for s in range(8):
                    for b in range(4):
                        kt = 1 + 4 * s + b
                        if st[s] > qst:
                            abias[:, g, kt, hb] = NEG
                        else:
                            kp = 512 * st[s] + 128 * b + kl
                            abias[:, g, kt, hb] = np.minimum((kp - qref) * slopes[hb], 40.0)
    mbias = np.zeros((128, 4, 9), np.float32)
    for j in range(4):
        qst = st[2 * j + 1]
        for s in range(8):
            if st[s] > qst:
                mbias[:, j, 1 + s] = NEG
    b2d = np.zeros((128, 9, 128), np.float32)
    k = np.arange(128)[:, None]
    q = np.arange(128)[None, :]
    vis = (k // 64) <= (q // 64)
    for hb in range(8):
        b2d[:, hb, :] = np.where(vis, -slopes[hb] * np.abs(q - k) + slopes[hb] * (q - 64), NEG)
    b2d[:, 8, :] = np.where(vis, 0.0, NEG)
    return dict(cosk=cosk, sink=sink, cosq=cosq, sinq=sinq, abias=abias.reshape(128, -1), mbias=mbias.reshape(128, -1),
                b2d=b2d.reshape(128, -1), ccon=ccon.reshape(128, -1))


def _colmajor(v, n):
    return np.ascontiguousarray(np.asarray(v, np.float32).reshape(n, 128).T)


def make_in_maps(x, meta_tokens, norm_pre, w_in, diff_lambda_q1, diff_lambda_k1, diff_lambda_q2, diff_lambda_k2,
                 diff_subln, mla_norm_q, mla_norm_kv, w_uq, w_ukv, w_out, norm_post):
    f = lambda a: np.ascontiguousarray(np.asarray(a, np.float32))
    x = f(x)
    meta = f(meta_tokens)
    shared = dict(
        w_in=f(w_in)[0], w_uq=f(w_uq)[0], w_ukv=f(w_ukv)[0], w_out=f(w_out)[0],
        gpre=_colmajor(norm_pre[0], 32), gcq=_colmajor(mla_norm_q[0], 12), gckv=_colmajor(mla_norm_kv[0], 4),
        gsub=_colmajor(diff_subln[0], 2), gpost=f(norm_post)[0:1],
        lam4=np.ascontiguousarray(np.stack([f(diff_lambda_q1)[0], f(diff_lambda_k1)[0], f(diff_lambda_q2)[0], f(diff_lambda_k2)[0]])),
        ident=np.eye(128, dtype=np.float32),
    )
    tabs = {0: _tables(0), 1: _tables(1)}
    maps = []
    for c in range(8):
        b, par = c // 2, c % 2
        st = SLOT_ST[par]
        xs = np.concatenate([meta] + [x[b, 512 * st[s]:512 * (st[s] + 1)] for s in range(8)], axis=0)
        m = dict(shared)
        m.update(tabs[par])
        m['xs'] = np.ascontiguousarray(xs)
        maps.append(m)
    return maps


_NC_CACHE = {}


def kernel(**inputs):
    in_maps = make_in_maps(**inputs)
    if 'nc' not in _NC_CACHE:
        _NC_CACHE['nc'] = build_nc()
    nc = _NC_CACHE['nc']
    res = run_bass_kernel_spmd(nc, in_maps, core_ids=list(range(8)))
    out = np.empty((4, 4096, 4096), np.float32)
    for c in range(8):
        b, par = c // 2, c % 2
        st = SLOT_ST[par]
        o = res.results[c]["out"]
        for j in range(4):
            s = st[2 * j + 1]
            out[b, 512 * s:512 * (s + 1)] = o[512 * j:512 * (j + 1)]
    return out
```

```python
from contextlib import ExitStack
import numpy as np
import concourse.bass as bass
import concourse.mybir as mybir
from concourse.bass_utils import run_bass_kernel_spmd

F32 = mybir.dt.float32
BF16 = mybir.dt.bfloat16
AF = mybir.ActivationFunctionType
ALU = mybir.AluOpType
AX = mybir.AxisListType

ENGS = ['pe', 'act', 'dve', 'pool', 'sp']
SEM_CAP = 12000
DEBUG = False


class T:
    __slots__ = ('name', 'w', 'r')

    def __init__(self, name=''):
        self.name = name
        self.w = None
        self.r = []


class Op:
    __slots__ = ('eng', 'fn', 'raw', 'oth', 'dma', 'key', 'sig', 'sem', 'val', 'name')


class Prog:
    def __init__(self, nc):
        self.nc = nc
        self.ops = {e: [] for e in ENGS}
        self.all_dma = []
        self.last = {e: None for e in ENGS}

    def add(self, eng, fn, reads=(), writes=(), dma=0, key=None, name='', ports=()):
        op = Op()
        op.eng, op.fn, op.dma, op.key, op.sig, op.name = eng, fn, dma, key, False, name
        op.sem = None
        op.val = 0
        raw, oth = set(), set()
        for t in reads:
            if t.w is not None:
                raw.add(t.w)
        for t in writes:
            if t.w is not None:
                oth.add(t.w)
            for r in t.r:
                oth.add(r)
        for t in reads:
            if not dma:
                t.r = [r for r in t.r if r.dma or r.eng != eng]
            t.r.append(op)
        for t in writes:
            t.w = op
            t.r = []
        for t in ports:
            if t.w is not None and t.w.eng != eng:
                oth.add(t.w)
            t.w = op
        raw.discard(op)
        oth.discard(op)
        op.raw, op.oth = raw, oth
        self.ops[eng].append(op)
        if dma:
            assert key is not None
            self.all_dma.append(op)
        else:
            self.last[eng] = op
        return op

    def barrier(self):
        deps = set(self.all_dma)
        for e in ENGS:
            if self.last[e] is not None:
                deps.add(self.last[e])
        for e in ENGS:
            op = Op()
            op.eng, op.fn, op.dma, op.key, op.sig, op.name = e, None, 0, None, False, 'barrier'
            op.sem, op.val = None, 0
            op.raw = set(deps)
            op.oth = set()
            self.ops[e].append(op)
        self.all_dma = []

    def _needed(self, op):
        out = []
        for d in op.raw:
            if d.fn is None:
                continue
            if (not d.dma) and d.eng == op.eng and op.eng == 'pe' and not op.dma:
                continue
            out.append(d)
        for d in op.oth:
            if d.fn is None:
                continue
            if (not d.dma) and d.eng == op.eng and op.eng == 'pe' and not op.dma:
                continue
            out.append(d)
        return out

    def emit(self, stack):
        nc = self.nc
        need = {}
        for e in ENGS:
            for op in self.ops[e]:
                nd = self._needed(op)
                need[id(op)] = nd
                for d in nd:
                    d.sig = True
        nsem = [0]

        def newsem(tag):
            nsem[0] += 1
            return stack.enter_context(nc.semaphore(f"s{nsem[0]}_{tag}"))

        keystate = {}
        for e in ENGS:
            cur = None
            cnt = 0
            for op in self.ops[e]:
                if op.fn is None:
                    continue
                if op.dma:
                    st = keystate.get(op.key)
                    if st is None or st[1] + 16 * op.dma > SEM_CAP:
                        st = [newsem('d'), 0]
                    st[1] += 16 * op.dma
                    keystate[op.key] = st
                    op.sem, op.val = st[0], st[1]
                    op.sig = True
                elif op.sig:
                    if cur is None or cnt + 1 > SEM_CAP:
                        cur = newsem(e)
                        cnt = 0
                    cnt += 1
                    op.sem, op.val = cur, cnt
        self.nsem = nsem[0]
        handles = {'pe': 'tensor', 'act': 'scalar', 'dve': 'vector', 'pool': 'gpsimd', 'sp': 'sync'}
        with nc.Block() as block:
            for e in ENGS:
                ops = self.ops[e]

                def body(eng, ops=ops):
                    waited = {}
                    for op in ops:
                        w = {}
                        for d in need[id(op)]:
                            k = d.sem.num
                            if k not in w or w[k][1] < d.val:
                                w[k] = (d.sem, d.val)
                        for k, (s, v) in w.items():
                            if waited.get(k, 0) < v:
                                eng.wait_ge(s, v)
                                waited[k] = v
                        if op.fn is None:
                            continue
                        ins = op.fn(eng)
                        if op.dma:
                            if not isinstance(ins, (list, tuple)):
                                ins = [ins]
                            assert len(ins) == op.dma, (op.name, len(ins), op.dma)
                            for i in ins:
                                i.then_inc(op.sem, 16)
                        elif op.sig:
                            assert ins is not None, op.name
                            ins.then_inc(op.sem, 1)

                getattr(block, handles[e])(body)


class Arena:
    def __init__(self, base_ap, nbytes):
        self.base = base_ap
        self.nbytes = nbytes
        self.off = 0

    def alloc(self, free_shape, dtype, name=''):
        isz = 4 if dtype == F32 else 2
        n = int(np.prod(free_shape))
        size = (n * isz + 63) // 64 * 64
        assert self.off + size <= self.nbytes, f"SBUF arena overflow at {name}: {self.off}+{size} > {self.nbytes}"
        a = self.base[:, self.off // 4:(self.off + size) // 4]
        self.off += size
        if dtype != F32:
            a = a.bitcast(dtype)
        a = a[:, 0:n]
        if len(free_shape) == 2:
            a = a.rearrange("p (a b) -> p a b", a=free_shape[0])
        elif len(free_shape) == 3:
            a = a.rearrange("p (a b c) -> p a b c", a=free_shape[0], b=free_shape[1])
        return a


class Rot:
    def __init__(self, items):
        self.items = items
        self.i = 0

    def next(self):
        it = self.items[self.i % len(self.items)]
        self.i += 1
        return it


D = 4096
KC = 32
NTOK = 4112
NOWN = 2048
INW = 12352
C_DQ, C_DK, C_DV, C_DG, C_CQ, C_CKV, C_KR, C_MG = 0, 2048, 4096, 6144, 8192, 9728, 10240, 10304
EPS = 1e-6
NEG = -30000.0
ARENA_BYTES = 204 * 1024
SLOT_ST = {0: [1, 0, 2, 3, 5, 4, 6, 7], 1: [0, 1, 3, 2, 4, 5, 7, 6]}


def build_nc():
    nc = bass.Bass("TRN2", target_bir_lowering=False)
    dk = "ExternalOutput" if DEBUG else "Internal"

    def din(name, shape, dt=F32):
        return nc.dram_tensor(name, list(shape), dt, kind="ExternalInput").ap()

    def dscr(name, shape, dt):
        return nc.dram_tensor(name, list(shape), dt, kind=dk).ap()

    xs = din("xs", [NTOK, D])
    w_in = din("w_in", [D, INW])
    w_uq = din("w_uq", [1536, 3072])
    w_ukv = din("w_ukv", [512, 4096])
    w_out = din("w_out", [D, D])
    gpre_d = din("gpre", [128, 32])
    gcq_d = din("gcq", [128, 12])
    gckv_d = din("gckv", [128, 4])
    gsub_d = din("gsub", [128, 2])
    gpost_d = din("gpost", [1, D])
    lam4_d = din("lam4", [4, 128])
    ident_d = din("ident", [128, 128])
    cosk_d = din("cosk", [128, NTOK])
    sink_d = din("sink", [128, NTOK])
    cosq_d = din("cosq", [128, NOWN])
    sinq_d = din("sinq", [128, NOWN])
    abias_d = din("abias", [128, 8 * 33 * 8])
    ccon_d = din("ccon", [128, 32])
    mbias_d = din("mbias", [128, 36])
    b2d_d = din("b2d", [128, 2 * 9 * 128])
    out_d = nc.dram_tensor("out", [NOWN, D], F32, kind="ExternalOutput").ap()

    QD = dscr("QD", [16, 128, NOWN], BF16)
    KD = dscr("KD", [16, 128, NTOK], BF16)
    VD = dscr("VD", [NTOK, 2048], BF16)
    GD = dscr("GD", [16, 128, NOWN], F32)
    QN = dscr("QN", [16, 128, NOWN], BF16)
    QR = dscr("QR", [8, 128, NOWN], BF16)
    KN = dscr("KN", [16, 128, NTOK], BF16)
    KR = dscr("KR", [128, NTOK], BF16)
    VM = dscr("VM", [NTOK, 2048], BF16)
    GM = dscr("GM", [16, 128, NOWN], F32)
    MIX = dscr("MIX", [32, 128, NOWN], BF16)
    YS = dscr("YS", [NOWN, D], F32)
    WB = nc.dram_tensor("WB", [64, 128, 8192], BF16, kind="Internal").ap()
    WOB = nc.dram_tensor("WOB", [8, 128, 16384], BF16, kind="Internal").ap()

    stack = ExitStack()
    with stack:
        arena_t = stack.enter_context(nc.sbuf_tensor("arena", [128, ARENA_BYTES // 4], F32))
        ps_t = stack.enter_context(nc.psum_tensor("ps", [128, 4096], F32))
        P = Prog(nc)
        bank = [ps_t[:, i * 512:(i + 1) * 512] for i in range(8)]
        Tb = [T(f"bank{i}") for i in range(8)]
        dram_T = {}

        def DT(*key):
            t = dram_T.get(key)
            if t is None:
                t = T(str(key))
                dram_T[key] = t
            return t

        A = Arena(arena_t[:], ARENA_BYTES)
        identf = A.alloc([128], F32)
        identb = A.alloc([128], BF16)
        onesb = A.alloc([128], BF16)
        onesf = A.alloc([128], F32)
        gpre = A.alloc([32], F32)
        gcq = A.alloc([12], F32)
        gckv = A.alloc([4], F32)
        gsub = A.alloc([2], F32)
        lamt = A.alloc([4, 128], F32)
        lamw = A.alloc([2, 128], F32)
        lams = A.alloc([8], F32)
        Tc = T("consts")
        Tlam = T("lam")
        const_base = A.off

        ci = [0]

        def cload(dst, src):
            ci[0] += 1
            P.add('sp', lambda e: e.dma_start(out=dst, in_=src), writes=[Tc], dma=1, key=f'c{ci[0]}')

        cload(identf, ident_d)
        cload(gpre, gpre_d)
        cload(gcq, gcq_d)
        cload(gckv, gckv_d)
        cload(gsub, gsub_d)
        for i in range(4):
            cload(lamt[:, i, :], lam4_d[i:i + 1, :].partition_broadcast(128))
        P.add('dve', lambda e: e.tensor_copy(out=identb, in_=identf), reads=[Tc], writes=[Tc])
        P.add('dve', lambda e: e.memset(onesb, 1.0), writes=[Tc])
        P.add('dve', lambda e: e.memset(onesf, 1.0), writes=[Tc])
        P.add('dve', lambda e: e.memset(lams, 0.0), writes=[Tlam])
        for i in range(2):
            P.add('dve', lambda e, i=i: e.tensor_mul(out=lamw[:, i, :], in0=lamt[:, 2 * i, :], in1=lamt[:, 2 * i + 1, :]),
                  reads=[Tc], writes=[Tlam])
            P.add('dve', lambda e, i=i: e.reduce_sum(out=lams[:, i:i + 1], in_=lamw[:, i, :], axis=AX.X),
                  reads=[Tlam], writes=[Tlam])
            P.add('act', lambda e, i=i: e.activation(out=lams[:, 2 + i:3 + i], in_=lams[:, i:i + 1], func=AF.Exp),
                  reads=[Tlam], writes=[Tlam])
        P.add('dve', lambda e: e.tensor_sub(out=lams[:, 4:5], in0=lams[:, 3:4], in1=lams[:, 2:3]), reads=[Tlam], writes=[Tlam])
        P.add('dve', lambda e: e.tensor_scalar_add(out=lams[:, 5:6], in0=lams[:, 4:5], scalar1=-0.2), reads=[Tlam], writes=[Tlam])
        P.add('dve', lambda e: e.tensor_scalar_mul(out=gsub, in0=gsub, scalar1=0.8), reads=[Tc], writes=[Tc])
        neglam = lams[:, 5:6]

        def rsqrt_tile(dst, src, n, inv_n, rd_T, wr_T, tmp, tmpT):
            P.add('dve', lambda e: e.tensor_scalar(out=tmp, in0=src, scalar1=inv_n, scalar2=EPS, op0=ALU.mult, op1=ALU.add),
                  reads=rd_T, writes=[tmpT])
            P.add('act', lambda e: e.activation(out=tmp, in_=tmp, func=AF.Sqrt), reads=[tmpT], writes=[tmpT])
            P.add('dve', lambda e: e.reciprocal(out=dst, in_=tmp), reads=[tmpT], writes=wr_T)

        uT = A.alloc([32, 1040], BF16, 'uT')
        TuT = T('uT')
        xst = [(A.alloc([1024], F32), T()) for _ in range(2)]
        xbf = [(A.alloc([1024], BF16), T()) for _ in range(2)]
        junk = A.alloc([1024], BF16)
        Tjunk = T()
        ssp = A.alloc([9, 4], F32)
        Tssp = T()
        rstd_tok = A.alloc([9], F32)
        rtmp = A.alloc([9], F32)
        Trstd = T()
        Rb = [(A.alloc([128], F32), T()) for _ in range(2)]
        Rpre = A.alloc([1040], F32)
        TRpre = T()
        wsl = [(A.alloc([32, 256], BF16), T()) for _ in range(2)]
        wrot = Rot(wsl)
        cqT = A.alloc([12, 512], BF16)
        TcqT = T()
        cqn = A.alloc([12, 512], BF16)
        Tcqn = T()
        ckvT = A.alloc([4, 1040], BF16)
        TckvT = T()
        ckvn = A.alloc([4, 1040], BF16)
        Tckvn = T()
        sqb = [(A.alloc([512], BF16), T()) for _ in range(3)]
        sqrot = Rot(sqb)
        Rn = A.alloc([1040], F32)
        TRn = T()
        Rtmp = A.alloc([512], F32)
        TRtmp = T()
        stb = [(A.alloc([512], BF16), T()) for _ in range(4)]
        stbrot = Rot(stb)
        stf = [(A.alloc([512], F32), T()) for _ in range(2)]
        wrA = Rot([(A.alloc([12, 128], BF16), T()) for _ in range(2)])
        wrB = Rot([(A.alloc([12, 128], BF16), T()) for _ in range(2)])
        stfrot = Rot(stf)
        ropek = [(A.alloc([512], F32), T()) for _ in range(2)]
        ropeq = [(A.alloc([512], F32), T()) for _ in range(2)]
        rt = [(A.alloc([512], F32), T()) for _ in range(2)]
        print("stage A arena bytes", A.off)
        mm_banks = Rot([0, 1, 2, 3, 4, 5])
        dmaq = Rot(['sp'])
        stn = [0]

        def store(eng, dst, src, srcT, dstTs):
            stn[0] += 1
            P.add(eng, lambda e: e.dma_start(out=dst, in_=src), reads=[srcT], writes=dstTs, dma=1,
                  key=('st', id(srcT)))

        def load_slab(src_ap_fn, ndma, view_fn=None):
            buf, Tw = wrot.next()
            P.add('pool', lambda e: src_ap_fn(e, buf), writes=[Tw], dma=ndma, key=('w', id(Tw)))
            return buf, Tw

        slab_ids = {}
        pending_wb = []
        cur_it = [0]

        def flush_wb():
            while pending_wb:
                pending_wb.pop(0)()

        def load_slab_cached(key, fn, ndma, post_fn=None):
            flush_wb()
            idx = slab_ids.setdefault(key, len(slab_ids))
            assert idx < 64
            cache_it = idx % 2
            if cur_it[0] <= cache_it:
                buf, Tw = load_slab(fn, ndma)
                if post_fn is not None:
                    post_fn(buf, Tw)
                flat = buf.rearrange("p a b -> p (a b)")
                if cur_it[0] == cache_it:
                    pending_wb.append(lambda: P.add('sp', lambda e: e.dma_start(out=WB[idx], in_=flat), reads=[Tw], writes=[DT('WB', idx)], dma=1, key=('wbst', id(Tw))))
            else:
                buf, Tw = wrot.next()
                flat = buf.rearrange("p a b -> p (a b)")
                P.add('pool', lambda e: e.dma_start(out=flat, in_=WB[idx]), reads=[DT('WB', idx)], writes=[Tw], dma=1, key=('w', id(Tw)))
            return buf, Tw

        def win_slab(c0):
            def fn(e, buf):
                src = w_in[:, c0:c0 + 256].rearrange("(k p) n -> p k n", p=128)
                return [e.dma_start(out=buf[:, 8 * i:8 * i + 8, :], in_=src[:, 8 * i:8 * i + 8, :]) for i in range(4)]
            return load_slab_cached(('win', c0), fn, 4)

        for it in range(4):
            cur_it[0] = it
            tok0 = 16 + 1024 * it
            own0 = 512 * it
            tts = [(0, 512), (512, 512)] + ([(1024, 16)] if it == 0 else [])

            def gtok(c):
                return (c - 1024) if c >= 1024 else tok0 + c

            blocks = [(tb * 128, 128) for tb in range(8)] + ([(1024, 16)] if it == 0 else [])
            for bi, (c0, nt) in enumerate(blocks):
                g0 = gtok(c0)
                P.add('dve', lambda e, bi=bi: e.memset(ssp[:, bi, :], 0.0), writes=[Tssp])
                for xc in range(4):
                    xi = (bi * 4 + xc) % 2
                    xa, Txa = xst[xi]
                    xb_, Txb = xbf[xi]
                    P.add('sp', lambda e, xa=xa, g0=g0, nt=nt, xc=xc: e.dma_start(out=xa[0:nt, :], in_=xs[g0:g0 + nt, xc * 1024:(xc + 1) * 1024]),
                          writes=[Txa], dma=1, key=('x', xi))
                    P.add('act', lambda e, xa=xa, nt=nt, bi=bi, xc=xc: e.activation(out=junk[0:nt, :], in_=xa[0:nt, :], func=AF.Square,
                                                                                   accum_out=ssp[0:nt, bi, xc:xc + 1]),
                          reads=[Txa], writes=[Tjunk, Tssp])
                    P.add('dve', lambda e, xa=xa, xb_=xb_, nt=nt: e.tensor_copy(out=xb_[0:nt, :], in_=xa[0:nt, :]), reads=[Txa], writes=[Txb])
                    tbk = 6 + ((bi * 4 + xc) % 2)
                    pb = bank[tbk].bitcast(BF16)
                    for i in range(8):
                        P.add('pe', lambda e, i=i, xb_=xb_, nt=nt, pb=pb: e.transpose(out=pb[:, i * 128:i * 128 + nt], in_=xb_[0:nt, i * 128:(i + 1) * 128],
                                                                                   identity=identb[0:nt, 0:nt]),
                              reads=[Txb, Tc], writes=[Tb[tbk]])
                    P.add('dve', lambda e, xc=xc, c0=c0, nt=nt, pb=pb: e.tensor_tensor(
                        out=uT[:, xc * 8:(xc + 1) * 8, c0:c0 + nt], in0=pb[:, 0:1024].rearrange("p (a b) -> p a b", a=8)[:, :, 0:nt],
                        in1=gpre[:, xc * 8:(xc + 1) * 8].unsqueeze(2).to_broadcast([128, 8, nt]), op=ALU.mult),
                          reads=[Tb[tbk], Tc], writes=[TuT])
                P.add('dve', lambda e, bi=bi, nt=nt: e.reduce_sum(out=rtmp[0:nt, bi:bi + 1], in_=ssp[0:nt, bi, :], axis=AX.X), reads=[Tssp], writes=[Trstd])
                P.add('dve', lambda e, bi=bi, nt=nt: e.tensor_scalar(out=rtmp[0:nt, bi:bi + 1], in0=rtmp[0:nt, bi:bi + 1], scalar1=1.0 / D, scalar2=EPS,
                                                                     op0=ALU.mult, op1=ALU.add), reads=[Trstd], writes=[Trstd])
                P.add('act', lambda e, bi=bi, nt=nt: e.activation(out=rtmp[0:nt, bi:bi + 1], in_=rtmp[0:nt, bi:bi + 1], func=AF.Sqrt), reads=[Trstd], writes=[Trstd])
                P.add('dve', lambda e, bi=bi, nt=nt: e.reciprocal(out=rstd_tok[0:nt, bi:bi + 1], in_=rtmp[0:nt, bi:bi + 1]), reads=[Trstd], writes=[Trstd])
                rb, Trb = Rb[bi % 2]
                P.add('dve', lambda e, rb=rb, bi=bi, nt=nt: e.tensor_scalar_mul(out=rb[0:nt, :], in0=onesf[0:nt, :], scalar1=rstd_tok[0:nt, bi:bi + 1]),
                      reads=[Trstd, Tc], writes=[Trb])
                rbk = mm_banks.next()
                P.add('pe', lambda e, rb=rb, nt=nt, rbk=rbk: e.matmul(bank[rbk][:, 0:nt], lhsT=rb[0:nt, :], rhs=identf[0:nt, 0:nt], start=True, stop=True),
                      reads=[Trb, Tc], writes=[Tb[rbk]])
                P.add('act', lambda e, c0=c0, nt=nt, rbk=rbk: e.activation(out=Rpre[:, c0:c0 + nt], in_=bank[rbk][:, 0:nt], func=AF.Copy), reads=[Tb[rbk]], writes=[TRpre])

            def fm_group(wbuf, Tw, wcol, KCn, rhs, Trhs, tiles, evac):
                for (c0, n) in tiles:
                    b = mm_banks.next()
                    for kc in range(KCn):
                        P.add('pe', lambda e, b=b, kc=kc, c0=c0, n=n: e.matmul(bank[b][:, 0:n], lhsT=wbuf[:, kc, wcol:wcol + 128], rhs=rhs[:, kc, c0:c0 + n],
                                                                               start=(kc == 0), stop=(kc == KCn - 1)),
                              reads=[Tw, Trhs], writes=[Tb[b]])
                    evac(b, c0, n)

            own_tile = [(512, 512)]

            def evac_bf16_store(dst_fn):
                def ev(b, c0, n):
                    sb, Tsb = stbrot.next()
                    P.add('dve', lambda e: e.tensor_tensor(out=sb[:, 0:n], in0=bank[b][:, 0:n], in1=Rpre[:, c0:c0 + n], op=ALU.mult),
                          reads=[Tb[b], TRpre], writes=[Tsb])
                    dst, dT = dst_fn(c0, n)
                    store(dmaq.next(), dst, sb[:, 0:n], Tsb, dT)
                return ev

            def evac_gate_store(dst_fn):
                def ev(b, c0, n):
                    sf, Tsf = stfrot.next()
                    P.add('dve', lambda e: e.tensor_tensor(out=sf[:, 0:n], in0=bank[b][:, 0:n], in1=Rpre[:, c0:c0 + n], op=ALU.mult),
                          reads=[Tb[b], TRpre], writes=[Tsf])
                    P.add('act', lambda e: e.activation(out=sf[:, 0:n], in_=sf[:, 0:n], func=AF.Silu), reads=[Tsf], writes=[Tsf])
                    dst, dT = dst_fn(c0, n)
                    store(dmaq.next(), dst, sf[:, 0:n], Tsf, dT)
                return ev

            def kr_fn(e, buf):
                src = w_in[:, C_KR:C_KR + 64].rearrange("(k p) n -> p k n", p=128)
                return [e.dma_start(out=buf[:, 8 * q4:8 * q4 + 8, 0:64], in_=src[:, 8 * q4:8 * q4 + 8, :]) for q4 in range(4)]
            def kr_post(wbuf, Tw):
                for (d0, s0, wdt) in ((64, 0, 64), (128, 32, 32), (160, 0, 32), (192, 128, 64)):
                    P.add('dve', lambda e, wbuf=wbuf, d0=d0, s0=s0, wdt=wdt: e.tensor_copy(out=wbuf[:, :, d0:d0 + wdt], in_=wbuf[:, :, s0:s0 + wdt]), reads=[Tw], writes=[Tw])
            wbuf, Tw = load_slab_cached(('kr',), kr_fn, 4, kr_post)
            for (c0, n) in tts:
                g0 = gtok(c0)
                (ck, Tck), (sk, Tsk) = ropek
                P.add('sp', lambda e, ck=ck, g0=g0, n=n: e.dma_start(out=ck[:, 0:n], in_=cosk_d[:, g0:g0 + n]), writes=[Tck], dma=1, key='ck')
                P.add('sp', lambda e, sk=sk, g0=g0, n=n: e.dma_start(out=sk[:, 0:n], in_=sink_d[:, g0:g0 + n]), writes=[Tsk], dma=1, key='sk')
                P.add('dve', lambda e, ck=ck, c0=c0, n=n: e.tensor_tensor(out=ck[:, 0:n], in0=ck[:, 0:n], in1=Rpre[:, c0:c0 + n], op=ALU.mult), reads=[Tck, TRpre], writes=[Tck])
                P.add('dve', lambda e, sk=sk, c0=c0, n=n: e.tensor_tensor(out=sk[:, 0:n], in0=sk[:, 0:n], in1=Rpre[:, c0:c0 + n], op=ALU.mult), reads=[Tsk, TRpre], writes=[Tsk])
                bA = mm_banks.next()
                bB = mm_banks.next()
                for kc in range(KC):
                    P.add('pe', lambda e, kc=kc, c0=c0, n=n, bA=bA, wbuf=wbuf: e.matmul(bank[bA][:, 0:n], lhsT=wbuf[:, kc, 0:128], rhs=uT[:, kc, c0:c0 + n], start=(kc == 0), stop=(kc == KC - 1)),
                          reads=[Tw, TuT], writes=[Tb[bA]])
                for kc in range(KC):
                    P.add('pe', lambda e, kc=kc, c0=c0, n=n, bB=bB, wbuf=wbuf: e.matmul(bank[bB][:, 0:n], lhsT=wbuf[:, kc, 128:256], rhs=uT[:, kc, c0:c0 + n], start=(kc == 0), stop=(kc == KC - 1)),
                          reads=[Tw, TuT], writes=[Tb[bB]])
                (t1, Tt1), (t2, Tt2) = rt
                P.add('dve', lambda e, t1=t1, bA=bA, ck=ck, n=n: e.tensor_tensor(out=t1[:, 0:n], in0=bank[bA][:, 0:n], in1=ck[:, 0:n], op=ALU.mult), reads=[Tb[bA], Tck], writes=[Tt1])
                P.add('dve', lambda e, t2=t2, bB=bB, sk=sk, n=n: e.tensor_tensor(out=t2[:, 0:n], in0=bank[bB][:, 0:n], in1=sk[:, 0:n], op=ALU.mult), reads=[Tb[bB], Tsk], writes=[Tt2])
                sb, Tsb = stbrot.next()
                P.add('dve', lambda e, sb=sb, t1=t1, t2=t2, n=n: e.tensor_tensor(out=sb[:, 0:n], in0=t1[:, 0:n], in1=t2[:, 0:n], op=ALU.add), reads=[Tt1, Tt2], writes=[Tsb])
                store('sp', KR[:, g0:g0 + n], sb[:, 0:n], Tsb, [DT('KR', g0)])

            for sl in range(2):
                wbuf, Tw = win_slab(C_CKV + 256 * sl)
                for blk in range(2):
                    kcq = sl * 2 + blk

                    def ev(b, c0, n, kcq=kcq):
                        P.add('dve', lambda e: e.tensor_tensor(out=ckvT[:, kcq, c0:c0 + n], in0=bank[b][:, 0:n], in1=Rpre[:, c0:c0 + n], op=ALU.mult),
                              reads=[Tb[b], TRpre], writes=[TckvT])
                    fm_group(wbuf, Tw, blk * 128, KC, uT, TuT, tts, ev)
            for sl in range(0, 1):
                wbuf, Tw = win_slab(C_DK + 256 * sl)
                for blk in range(2):
                    m = sl * 2 + blk
                    fm_group(wbuf, Tw, blk * 128, KC, uT, TuT, tts,
                             evac_bf16_store(lambda c0, n, m=m: (KD[m][:, gtok(c0):gtok(c0) + n], [DT('KD', m, gtok(c0))])))
            for (c0, n) in tts:
                b = mm_banks.next()
                for kcq in range(4):
                    sq, Tsq = sqrot.next()
                    P.add('act', lambda e, sq=sq, kcq=kcq, c0=c0, n=n: e.activation(out=sq[:, 0:n], in_=ckvT[:, kcq, c0:c0 + n], func=AF.Square), reads=[TckvT], writes=[Tsq])
                    P.add('pe', lambda e, sq=sq, kcq=kcq, n=n, b=b: e.matmul(bank[b][:, 0:n], lhsT=onesb, rhs=sq[:, 0:n], start=(kcq == 0), stop=(kcq == 3)),
                          reads=[Tsq, Tc], writes=[Tb[b]])
                rsqrt_tile(Rn[:, c0:c0 + n], bank[b][:, 0:n], n, 1.0 / 512, [Tb[b]], [TRn], Rtmp[:, 0:n], TRtmp)
                for kcq in range(4):
                    P.add('dve', lambda e, kcq=kcq, c0=c0, n=n: e.scalar_tensor_tensor(out=ckvn[:, kcq, c0:c0 + n], in0=ckvT[:, kcq, c0:c0 + n], scalar=gckv[:, kcq:kcq + 1],
                                                                                       in1=Rn[:, c0:c0 + n], op0=ALU.mult, op1=ALU.mult),
                          reads=[TckvT, TRn, Tc], writes=[Tckvn])
            for sl in range(1, 8):
                wbuf, Tw = win_slab(C_DK + 256 * sl)
                for blk in range(2):
                    m = sl * 2 + blk
                    fm_group(wbuf, Tw, blk * 128, KC, uT, TuT, tts,
                             evac_bf16_store(lambda c0, n, m=m: (KD[m][:, gtok(c0):gtok(c0) + n], [DT('KD', m, gtok(c0))])))
            for half in range(2):
                def ukv_fn(e, buf, half=half):
                    v = buf.rearrange("p a b -> p (a b)").rearrange("p (k n) -> p k n", k=4)
                    src = w_ukv[:, half * 2048:(half + 1) * 2048].rearrange("(k p) n -> p k n", p=128)
                    return [e.dma_start(out=v[:, :, 1024 * i:1024 * i + 1024], in_=src[:, :, 1024 * i:1024 * i + 1024]) for i in range(2)]
                wbuf, Tw = load_slab_cached(('ukv', half), ukv_fn, 2)
                wv = wbuf.rearrange("p a b -> p (a b)").rearrange("p (k n) -> p k n", k=4)
                for hl in range(8):
                    h = half * 8 + hl

                    def ev(b, c0, n, h=h):
                        sb, Tsb = stbrot.next()
                        P.add('act', lambda e: e.activation(out=sb[:, 0:n], in_=bank[b][:, 0:n], func=AF.Copy), reads=[Tb[b]], writes=[Tsb])
                        g0 = gtok(c0)
                        store(dmaq.next(), KN[h][:, g0:g0 + n], sb[:, 0:n], Tsb, [DT('KN', h, g0)])
                    fm_group(wv, Tw, hl * 256, 4, ckvn, Tckvn, tts, ev)
                for (c0, nt) in blocks:
                    g0 = gtok(c0)
                    for hg in range(2):
                        b = mm_banks.next()
                        for i4 in range(4):
                            hl = hg * 4 + i4
                            for kcq in range(4):
                                P.add('pe', lambda e, b=b, i4=i4, hl=hl, kcq=kcq, c0=c0, nt=nt, wv=wv: e.matmul(bank[b][0:nt, i4 * 128:(i4 + 1) * 128], lhsT=ckvn[:, kcq, c0:c0 + nt],
                                                                                                           rhs=wv[:, kcq, hl * 256 + 128:hl * 256 + 256], start=(kcq == 0), stop=(kcq == 3)),
                                      reads=[Tw, Tckvn], writes=[Tb[b]])
                        sb, Tsb = stbrot.next()
                        P.add('act', lambda e, sb=sb, b=b, nt=nt: e.activation(out=sb[0:nt, :], in_=bank[b][0:nt, :], func=AF.Copy), reads=[Tb[b]], writes=[Tsb])
                        cc = half * 1024 + hg * 512
                        store(dmaq.next(), VM[g0:g0 + nt, cc:cc + 512], sb[0:nt, :], Tsb, [DT('VM', g0, cc)])

            for sl in range(8):
                wbuf, Tw = win_slab(C_DV + 256 * sl)
                for bi, (c0, nt) in enumerate(blocks):
                    g0 = gtok(c0)
                    b = mm_banks.next()
                    for kc in range(KC):
                        P.add('pe', lambda e, b=b, kc=kc, c0=c0, nt=nt, wbuf=wbuf: e.matmul(bank[b][0:nt, 0:256], lhsT=uT[:, kc, c0:c0 + nt], rhs=wbuf[:, kc, :],
                                                                                        start=(kc == 0), stop=(kc == KC - 1)),
                              reads=[Tw, TuT], writes=[Tb[b]])
                    sb, Tsb = stbrot.next()
                    P.add('act', lambda e, sb=sb, b=b, nt=nt, bi=bi: e.activation(out=sb[0:nt, 0:256], in_=bank[b][0:nt, 0:256], func=AF.Copy, scale=rstd_tok[0:nt, bi:bi + 1]),
                          reads=[Tb[b], Trstd], writes=[Tsb])
                    store(dmaq.next(), VD[g0:g0 + nt, sl * 256:(sl + 1) * 256], sb[0:nt, 0:256], Tsb, [DT('VD', g0, sl)])
            for sl in range(6):
                wbuf, Tw = win_slab(C_CQ + 256 * sl)
                for blk in range(2):
                    kcq = sl * 2 + blk

                    def ev(b, c0, n, kcq=kcq):
                        P.add('dve', lambda e: e.tensor_tensor(out=cqT[:, kcq, :], in0=bank[b][:, 0:n], in1=Rpre[:, c0:c0 + n], op=ALU.mult),
                              reads=[Tb[b], TRpre], writes=[TcqT])
                    fm_group(wbuf, Tw, blk * 128, KC, uT, TuT, own_tile, ev)
            for sl in range(0, 1):
                wbuf, Tw = win_slab(C_DQ + 256 * sl)
                for blk in range(2):
                    m = sl * 2 + blk
                    fm_group(wbuf, Tw, blk * 128, KC, uT, TuT, own_tile,
                             evac_bf16_store(lambda c0, n, m=m: (QD[m][:, own0:own0 + 512], [DT('QD', m, own0)])))
            b = mm_banks.next()
            for kcq in range(12):
                sq, Tsq = sqrot.next()
                P.add('act', lambda e, sq=sq, kcq=kcq: e.activation(out=sq, in_=cqT[:, kcq, :], func=AF.Square), reads=[TcqT], writes=[Tsq])
                P.add('pe', lambda e, sq=sq, kcq=kcq, b=b: e.matmul(bank[b], lhsT=onesb, rhs=sq, start=(kcq == 0), stop=(kcq == 11)), reads=[Tsq, Tc], writes=[Tb[b]])
            rsqrt_tile(Rn[:, 0:512], bank[b], 512, 1.0 / 1536, [Tb[b]], [TRn], Rtmp, TRtmp)
            for kcq in range(12):
                P.add('dve', lambda e, kcq=kcq: e.scalar_tensor_tensor(out=cqn[:, kcq, :], in0=cqT[:, kcq, :], scalar=gcq[:, kcq:kcq + 1], in1=Rn[:, 0:512],
                                                                       op0=ALU.mult, op1=ALU.mult), reads=[TcqT, TRn, Tc], writes=[Tcqn])
            for sl in range(1, 8):
                wbuf, Tw = win_slab(C_DQ + 256 * sl)
                for blk in range(2):
                    m = sl * 2 + blk
                    fm_group(wbuf, Tw, blk * 128, KC, uT, TuT, own_tile,
                             evac_bf16_store(lambda c0, n, m=m: (QD[m][:, own0:own0 + 512], [DT('QD', m, own0)])))
            cqn_tile = [(0, 512)]
            (cq_, Tcq_), (sq_, Tsq_) = ropeq
            P.add('sp', lambda e, cq_=cq_, own0=own0: e.dma_start(out=cq_, in_=cosq_d[:, own0:own0 + 512]), writes=[Tcq_], dma=1, key='cq')
            P.add('sp', lambda e, sq_=sq_, own0=own0: e.dma_start(out=sq_, in_=sinq_d[:, own0:own0 + 512]), writes=[Tsq_], dma=1, key='sq')
            for pair in range(8):
                def uq_fn(e, buf, pair=pair):
                    v = buf.rearrange("p a b -> p (a b)")[:, 0:4608].rearrange("p (k n) -> p k n", k=12)
                    src = w_uq[:, pair * 384:(pair + 1) * 384].rearrange("(k p) n -> p k n", p=128)
                    return [e.dma_start(out=v, in_=src)]
                wbuf, Tw = load_slab_cached(('uq', pair), uq_fn, 1)
                wv = wbuf.rearrange("p a b -> p (a b)")[:, 0:4608].rearrange("p (k n) -> p k n", k=12)
                wa, Twa = wrA.next()
                wb, Twb = wrB.next()
                for i in range(2):
                    c1 = i * 192 + 128
                    P.add('dve', lambda e, wa=wa, wv=wv, i=i, c1=c1: e.tensor_copy(out=wa[:, :, i * 64:(i + 1) * 64], in_=wv[:, :, c1:c1 + 64]), reads=[Tw], writes=[Twa])
                    P.add('dve', lambda e, wb=wb, wv=wv, i=i, c1=c1: e.tensor_copy(out=wb[:, :, i * 64:i * 64 + 32], in_=wv[:, :, c1 + 32:c1 + 64]), reads=[Tw], writes=[Twb])
                    P.add('dve', lambda e, wb=wb, wv=wv, i=i, c1=c1: e.tensor_copy(out=wb[:, :, i * 64 + 32:i * 64 + 64], in_=wv[:, :, c1:c1 + 32]), reads=[Tw], writes=[Twb])
                for i in range(2):
                    h = pair * 2 + i

                    def ev(b, c0, n, h=h):
                        sb, Tsb = stbrot.next()
                        P.add('act', lambda e: e.activation(out=sb, in_=bank[b], func=AF.Copy), reads=[Tb[b]], writes=[Tsb])
                        store(dmaq.next(), QN[h][:, own0:own0 + 512], sb, Tsb, [DT('QN', h, own0)])
                    fm_group(wv, Tw, i * 192, 12, cqn, Tcqn, cqn_tile, ev)
                bA = mm_banks.next()
                bB = mm_banks.next()
                for kcq in range(12):
                    P.add('pe', lambda e, kcq=kcq, bA=bA, wa=wa: e.matmul(bank[bA], lhsT=wa[:, kcq, :], rhs=cqn[:, kcq, :], start=(kcq == 0), stop=(kcq == 11)),
                          reads=[Twa, Tcqn], writes=[Tb[bA]])
                for kcq in range(12):
                    P.add('pe', lambda e, kcq=kcq, bB=bB, wb=wb: e.matmul(bank[bB], lhsT=wb[:, kcq, :], rhs=cqn[:, kcq, :], start=(kcq == 0), stop=(kcq == 11)),
                          reads=[Twb, Tcqn], writes=[Tb[bB]])
                (t1, Tt1), (t2, Tt2) = rt
                P.add('dve', lambda e, t1=t1, bA=bA, cq_=cq_: e.tensor_tensor(out=t1, in0=bank[bA], in1=cq_, op=ALU.mult), reads=[Tb[bA], Tcq_], writes=[Tt1])
                P.add('dve', lambda e, t2=t2, bB=bB, sq_=sq_: e.tensor_tensor(out=t2, in0=bank[bB], in1=sq_, op=ALU.mult), reads=[Tb[bB], Tsq_], writes=[Tt2])
                sb, Tsb = stbrot.next()
                P.add('dve', lambda e, sb=sb, t1=t1, t2=t2: e.tensor_tensor(out=sb, in0=t1, in1=t2, op=ALU.add), reads=[Tt1, Tt2], writes=[Tsb])
                store(dmaq.next(), QR[pair][:, own0:own0 + 512], sb, Tsb, [DT('QR', pair, own0)])
            for sl in range(8):
                wbuf, Tw = win_slab(C_DG + 256 * sl)
                for blk in range(2):
                    m = sl * 2 + blk
                    fm_group(wbuf, Tw, blk * 128, KC, uT, TuT, own_tile,
                             evac_gate_store(lambda c0, n, m=m: (GD[m][:, own0:own0 + 512], [DT('GD', m, own0)])))
            for sl in range(8):
                wbuf, Tw = win_slab(C_MG + 256 * sl)
                for blk in range(2):
                    m = sl * 2 + blk
                    fm_group(wbuf, Tw, blk * 128, KC, uT, TuT, own_tile,
                             evac_gate_store(lambda c0, n, m=m: (GM[m][:, own0:own0 + 512], [DT('GM', m, own0)])))
            flush_wb()

        P.barrier()
        if STAGES >= 2:
            stage_b(nc, P, A, const_base, bank, Tb, dict(
                identb=identb, onesb=onesb, Tc=Tc, neglam=neglam, Tlam=Tlam, gsub=gsub,
                abias_d=abias_d, mbias_d=mbias_d, b2d_d=b2d_d, ccon_d=ccon_d,
                QD=QD, KD=KD, VD=VD, GD=GD, QN=QN, QR=QR, KN=KN, KR=KR, VM=VM, GM=GM, MIX=MIX, w_out=w_out, WOB=WOB))
            P.barrier()
        if STAGES >= 3:
            stage_c(nc, P, A, const_base, bank, Tb, dict(MIX=MIX, YS=YS, w_out=w_out, xs=xs, gpost_d=gpost_d, out_d=out_d, WOB=WOB))
        else:
            A.off = const_base
            z = A.alloc([4096], F32)
            Tz = T()
            P.add('dve', lambda e: e.memset(z, 0.0), writes=[Tz])
            To = T()
            for i in range(16):
                P.add('sp', lambda e, i=i: e.dma_start(out=out_d[i * 128:(i + 1) * 128, :], in_=z), reads=[Tz], writes=[To], dma=1, key='oz')
            P.barrier()
        P.emit(stack)
        print("semaphores used:", P.nsem, "ops:", {e: len(P.ops[e]) for e in ENGS})
    return nc


STAGES = 3


def stage_b(nc, P, A, const_base, bank, Tb, c):
    identb, onesb, Tc, neglam, Tlam, gsub = c['identb'], c['onesb'], c['Tc'], c['neglam'], c['Tlam'], c['gsub']
    QD, KD, VD, GD, QN, QR, KN, KR, VM, GM, MIX = (c[k] for k in ['QD', 'KD', 'VD', 'GD', 'QN', 'QR', 'KN', 'KR', 'VM', 'GM', 'MIX'])
    A.off = const_base
    abias = A.alloc([8 * 33 * 8], F32)
    mbias = A.alloc([36], F32)
    ccon = A.alloc([32], F32)
    b2d = A.alloc([2 * 9 * 128], BF16)
    KRT = A.alloc([NTOK], BF16)
    KRT2 = A.alloc([NTOK], BF16)
    epsc = A.alloc([1], F32)
    Tt = T('tables')
    P.add('sp', lambda e: e.dma_start(out=abias, in_=c['abias_d']), writes=[Tt], dma=1, key='tb1')
    P.add('sp', lambda e: e.dma_start(out=mbias, in_=c['mbias_d']), writes=[Tt], dma=1, key='tb2')
    P.add('pool', lambda e: e.dma_start(out=b2d, in_=c['b2d_d']), writes=[Tt], dma=1, key='tb3')
    P.add('sp', lambda e: e.dma_start(out=KRT, in_=KR), writes=[Tt], dma=1, key='tb4')
    P.add('sp', lambda e: e.dma_start(out=KRT2, in_=KR), writes=[Tt], dma=1, key='tb6')
    P.add('dve', lambda e: e.memset(KRT[64:128, :], 0.0), reads=[Tt], writes=[Tt])
    P.add('dve', lambda e: e.memset(KRT2[0:64, :], 0.0), reads=[Tt], writes=[Tt])
    P.add('dve', lambda e: e.memset(epsc, EPS), writes=[Tt])
    P.add('sp', lambda e: e.dma_start(out=ccon, in_=c['ccon_d']), writes=[Tt], dma=1, key='tb5')
    KTb = Rot([(A.alloc([NTOK], BF16), T()) for _ in range(4)])
    Vb = Rot([(A.alloc([32, 256], BF16), A.alloc([256], BF16), T()) for _ in range(2)])
    QTb = Rot([(A.alloc([NOWN], BF16), T()) for _ in range(4)])
    QRb = Rot([(A.alloc([NOWN], BF16), T()) for _ in range(2)])
    PT = Rot([(A.alloc([512], BF16), [T() for _ in range(4)]) for _ in range(4)])
    tmp2 = Rot([(A.alloc([128], F32), T()) for _ in range(2)])
    On = [(A.alloc([2, 512], F32), T()) for _ in range(4)]
    rden = Rot([(A.alloc([512], F32), T()) for _ in range(2)])
    gate = Rot([(A.alloc([512], F32), T()) for _ in range(4)])
    sqo = A.alloc([2, 512], BF16)
    Tsqo = T()
    rs = A.alloc([512], F32)
    Trs = T()
    rstmp = A.alloc([512], F32)
    Trstmp = T()
    mst = Rot([(A.alloc([512], BF16), T()) for _ in range(3)])
    wost = A.alloc([32, 512], BF16)
    Twost = T()
    print("stage B arena bytes", A.off)
    Sb = Rot([0, 1, 2])
    Tport = {0: T('port0'), 1: T('port1'), 2: T('port2')}
    Ob = Rot([(3, 4), (5, 6)])
    Db = Rot([7])
    dsb = Rot([(A.alloc([512], F32), T()) for _ in range(2)])

    def load_kt(src):
        buf, Tk = KTb.next()
        P.add('pool', lambda e: e.dma_start(out=buf, in_=src), writes=[Tk], dma=1, key=('kt', id(Tk)))
        return buf, Tk

    def load_qt(src, rot):
        buf, Tq = rot.next()
        P.add('pool', lambda e: e.dma_start(out=buf, in_=src), writes=[Tq], dma=1, key=('qt', id(Tq)))
        return buf, Tq

    def load_v(src, col0, E):
        vb, vm, Tv = Vb.next()
        s3 = src[16:NTOK, col0:col0 + E].rearrange("(n p) e -> p n e", p=128)

        def fn(e):
            ins = [e.dma_start(out=vb[:, 8 * i:8 * i + 8, 0:E], in_=s3[:, 8 * i:8 * i + 8, :]) for i in range(4)]
            ins.append(e.dma_start(out=vm[0:16, 0:E], in_=src[0:16, col0:col0 + E]))
            return ins
        P.add('pool', fn, writes=[Tv], dma=5, key=('v', id(Tv)))
        return vb, vm, Tv

    tiles = []

    def make_unit(src, E, hb, scale, j, epilogue, pre=None):
        neh = E // 128
        ust = {}
        seq = [('meta', 0, 0)] + [('full', s, b) for s in range(2 * j + 1) for b in range(4)] + [('diag', 2 * j + 1, b) for b in range(4)]
        for idx, (cls, s, b) in enumerate(seq):
            first = idx == 0
            last = idx == len(seq) - 1
            if cls == 'meta':
                k0, nk, q0, ktile = 0, 16, 0, 0
            else:
                k0, nk, ktile = 16 + 512 * s + 128 * b, 128, 1 + 4 * s + b
                q0 = 128 * b if cls == 'diag' else 0
            st = {}

            def qk(st=st, k0=k0, nk=nk, q0=q0, first=first, cls=cls):
                if first:
                    ust['ob'] = Ob.next()
                    ust['db'] = Db.next()
                KT, Tk, QT, Tq = src['KT'], src['Tk'], src['QT'], src['Tq']
                rope = src.get('rope')
                sbk = Sb.next()
                st['sbk'] = sbk
                S = bank[sbk]
                diag = cls == 'diag'
                P.add('pe', lambda e: e.matmul(S[0:nk, q0:512], lhsT=KT[:, k0:k0 + nk], rhs=QT[:, 512 * j + q0:512 * j + 512], start=True, stop=(rope is None and not diag)),
                      reads=[Tk, Tq], writes=[Tb[sbk]])
                if rope is not None:
                    QRT, Tqr, pr = rope
                    KRx = KRT if pr == 0 else KRT2
                    P.add('pe', lambda e: e.matmul(S[0:nk, q0:512], lhsT=KRx[:, k0:k0 + nk], rhs=QRT[:, 512 * j + q0:512 * j + 512], start=False, stop=(not diag)),
                          reads=[Tt, Tqr], writes=[Tb[sbk]])
                if diag:
                    for part in range(2):
                        c0 = part * 9 * 128 + hb * 128
                        P.add('pe', lambda e, part=part, c0=c0: e.matmul(S[:, q0:q0 + 128], lhsT=identb, rhs=b2d[:, c0:c0 + 128], start=False, stop=(part == 1)),
                              reads=[Tt, Tc], writes=[Tb[sbk]])

            def ex(st=st, cls=cls, s=s, b=b, nk=nk, q0=q0, ktile=ktile):
                sbk = st['sbk']
                S = bank[sbk]
                pt, Tpts = PT.next()
                st['pt'], st['Tpts'] = pt, Tpts
                a0 = q0 // 128
                a_lo = a0

                def seg_T(lo, hi):
                    return [Tpts[a] for a in range(lo // 128, (hi + 127) // 128)]
                if cls == 'diag':
                    if hb == 8:
                        bias_d = mbias[:, j * 9:j * 9 + 1]
                    else:
                        cc = a0 * 8 + hb
                        bias_d = ccon[:, cc:cc + 1]
                    P.add('act', lambda e: e.activation(out=pt[:, a0 * 128:(a0 + 1) * 128], in_=S[:, a0 * 128:(a0 + 1) * 128], func=AF.Exp, bias=bias_d, scale=scale),
                          reads=[Tb[sbk], Tt], writes=[Tpts[a0]], ports=[Tport[sbk]])
                    a_lo = a0 + 1
                lo = a_lo * 128
                if lo < 512:
                    if hb == 8:
                        col = j * 9 + (0 if cls == 'meta' else 1 + s)
                        segs = [(lo, 512, mbias[0:nk, col:col + 1])]
                    elif hb < 2:
                        segs = []
                        for hf in range(2):
                            l2, h2 = max(lo, 256 * hf), 256 * (hf + 1)
                            if l2 < h2:
                                col = ((2 * j + hf) * 33 + ktile) * 8 + hb
                                segs.append((l2, h2, abias[0:nk, col:col + 1]))
                    else:
                        col = ((2 * j) * 33 + ktile) * 8 + hb
                        segs = [(lo, 512, abias[0:nk, col:col + 1])]
                    for (l2, h2, bias_ap) in segs:
                        P.add('act', lambda e, l2=l2, h2=h2, bias_ap=bias_ap: e.activation(out=pt[0:nk, l2:h2], in_=S[0:nk, l2:h2], func=AF.Exp, bias=bias_ap, scale=scale),
                              reads=[Tb[sbk], Tt], writes=seg_T(l2, h2), ports=[Tport[sbk]])

            def pv(st=st, cls=cls, s=s, b=b, nk=nk, q0=q0, first=first, last=last):
                pt, Tpts = st['pt'], st['Tpts']
                vb, vm, Tv = src['vb'], src['vm'], src['Tv']
                ob, db = ust['ob'], ust['db']
                rT = [Tpts[a] for a in range(q0 // 128, 4)]
                for eh in range(neh):
                    lhsT = vm[0:16, eh * 128:(eh + 1) * 128] if cls == 'meta' else vb[:, 4 * s + b, eh * 128:(eh + 1) * 128]
                    P.add('pe', lambda e, eh=eh, lhsT=lhsT: e.matmul(bank[ob[eh]][:, q0:512], lhsT=lhsT, rhs=pt[0:nk, q0:512], start=first, stop=last),
                          reads=[Tv] + rT, writes=[Tb[ob[eh]]])
                P.add('pe', lambda e: e.matmul(bank[db][:, q0:512], lhsT=onesb[0:nk, :], rhs=pt[0:nk, q0:512], start=first, stop=last),
                      reads=[Tc] + rT, writes=[Tb[db]])

            tiles.append(dict(qk=qk, ex=ex, pv=pv, post=None, pre=(pre if first else None)))
        tiles[-1]['post'] = lambda: epilogue(ust['ob'], ust['db'])

    def normalize(ob, db, neh, dst, Tdst):
        rd, Trd = rden.next()
        ds, Tds = dsb.next()
        P.add('act', lambda e: e.activation(out=ds, in_=bank[db], func=AF.Copy), reads=[Tb[db]], writes=[Tds])
        P.add('dve', lambda e: e.reciprocal(out=rd, in_=ds), reads=[Tds], writes=[Trd])
        for eh in range(neh):
            P.add('dve', lambda e, eh=eh: e.tensor_tensor(out=dst[:, eh, :], in0=bank[ob[eh]], in1=rd, op=ALU.mult), reads=[Tb[ob[eh]], Trd], writes=[Tdst])

    sc_d = 128.0 ** -0.5
    sc_m = 192.0 ** -0.5
    oni = [0]
    for h in range(8):
        src0, src1 = {}, {}

        def pre_d(h=h, src0=src0, src1=src1):
            vb, vm, Tv = load_v(VD, h * 256, 256)
            for m, sr in ((0, src0), (1, src1)):
                sr['vb'], sr['vm'], sr['Tv'] = vb, vm, Tv
                sr['KT'], sr['Tk'] = load_kt(KD[2 * h + m])
                sr['QT'], sr['Tq'] = load_qt(QD[2 * h + m], QTb)
        for j in range(4):
            o0, To0 = On[oni[0] % 4]
            o1, To1 = On[(oni[0] + 1) % 4]
            oni[0] += 2

            def epi0(ob, db, o0=o0, To0=To0):
                normalize(ob, db, 2, o0, To0)
                return None

            def epi1(ob, db, o0=o0, To0=To0, o1=o1, To1=To1, h=h, j=j):
                normalize(ob, db, 2, o1, To1)
                for eh in range(2):
                    P.add('dve', lambda e, eh=eh: e.scalar_tensor_tensor(out=o0[:, eh, :], in0=o1[:, eh, :], scalar=neglam, in1=o0[:, eh, :], op0=ALU.mult, op1=ALU.add),
                          reads=[To0, To1, Tlam], writes=[To0])
                    P.add('dve', lambda e, eh=eh: e.tensor_tensor(out=sqo[:, eh, :], in0=o0[:, eh, :], in1=o0[:, eh, :], op=ALU.mult), reads=[To0], writes=[Tsqo])

                def part2():
                    sbk = Sb.next()
                    Sb.next()
                    Sb.next()
                    for eh in range(2):
                        P.add('pe', lambda e, eh=eh: e.matmul(bank[sbk], lhsT=onesb, rhs=sqo[:, eh, :], start=(eh == 0), stop=(eh == 1)), reads=[Tsqo, Tc], writes=[Tb[sbk]])
                    P.add('act', lambda e: e.activation(out=rstmp, in_=bank[sbk], func=AF.Ln, bias=epsc[:, 0:1], scale=1.0 / 256), reads=[Tb[sbk], Tt], writes=[Trstmp], ports=[Tport[sbk]])
                    P.add('act', lambda e: e.activation(out=rs, in_=rstmp, func=AF.Exp, scale=-0.5), reads=[Trstmp], writes=[Trs])
                    for eh in range(2):
                        g, Tg = gate.next()
                        ch = 2 * h + eh
                        P.add('sp', lambda e, g=g, ch=ch: e.dma_start(out=g, in_=GD[ch][:, 512 * j:512 * j + 512]), writes=[Tg], dma=1, key=('g', id(Tg)))
                        P.add('dve', lambda e, eh=eh: e.tensor_tensor(out=o0[:, eh, :], in0=o0[:, eh, :], in1=rs, op=ALU.mult), reads=[To0, Trs], writes=[To0])
                        ms, Tms = mst.next()
                        P.add('dve', lambda e, eh=eh, g=g, ms=ms: e.scalar_tensor_tensor(out=ms, in0=o0[:, eh, :], scalar=gsub[:, eh:eh + 1], in1=g, op0=ALU.mult, op1=ALU.mult),
                              reads=[To0, Tg, Tc], writes=[Tms])
                        P.add('sp', lambda e, ms=ms, ch=ch: e.dma_start(out=MIX[ch][:, 512 * j:512 * j + 512], in_=ms), reads=[Tms], writes=[T()], dma=1, key=('ms', id(Tms)))
                return part2
            make_unit(src0, 256, h, sc_d, j, epi0, pre=(pre_d if j == 0 else None))
            make_unit(src1, 256, h, sc_d, j, epi1)
    qr_state = {}
    for h in range(16):
        srcm = {}

        def pre_m(h=h, srcm=srcm):
            srcm['vb'], srcm['vm'], srcm['Tv'] = load_v(VM, h * 128, 128)
            srcm['KT'], srcm['Tk'] = load_kt(KN[h])
            srcm['QT'], srcm['Tq'] = load_qt(QN[h], QTb)
            if h % 2 == 0:
                qr_state['QRT'], qr_state['Tqr'] = load_qt(QR[h // 2], QRb)
            srcm['rope'] = (qr_state['QRT'], qr_state['Tqr'], 64 * (h % 2))
            if h >= 1 and h <= 8:
                ns = h - 1
                w_out, WOB = c['w_out'], c['WOB']

                def wfn(e, ns=ns):
                    srcw = w_out[:, ns * 512:(ns + 1) * 512].rearrange("(k p) n -> p k n", p=128)
                    return [e.dma_start(out=wost[:, 8 * i:8 * i + 8, :], in_=srcw[:, 8 * i:8 * i + 8, :]) for i in range(4)]
                P.add('pool', wfn, writes=[Twost], dma=4, key='wost')
                P.add('pool', lambda e, ns=ns: e.dma_start(out=WOB[ns], in_=wost.rearrange("p a b -> p (a b)")), reads=[Twost], writes=[T()], dma=1, key='wost2')
        for j in range(4):
            o0, To0 = On[oni[0] % 4]
            oni[0] += 1

            def epim(ob, db, o0=o0, To0=To0, h=h, j=j):
                normalize(ob, db, 1, o0, To0)
                g, Tg = gate.next()
                P.add('sp', lambda e: e.dma_start(out=g, in_=GM[h][:, 512 * j:512 * j + 512]), writes=[Tg], dma=1, key=('g', id(Tg)))
                ms, Tms = mst.next()
                P.add('dve', lambda e: e.tensor_tensor(out=ms, in0=o0[:, 0, :], in1=g, op=ALU.mult), reads=[To0, Tg], writes=[Tms])
                P.add('sp', lambda e: e.dma_start(out=MIX[16 + h][:, 512 * j:512 * j + 512], in_=ms), reads=[Tms], writes=[T()], dma=1, key=('ms', id(Tms)))
                return None
            make_unit(srcm, 128, 8, sc_m, j, epim, pre=(pre_m if j == 0 else None))

    deferred = []
    n = len(tiles)

    def start(t):
        if t['pre'] is not None:
            t['pre']()
        t['qk']()
    LOOK = 2
    for i in range(min(LOOK, n)):
        start(tiles[i])
    for i, t in enumerate(tiles):
        if i + LOOK < n:
            start(tiles[i + LOOK])
        t['ex']()
        t['pv']()
        while deferred and deferred[0][0] <= i:
            deferred.pop(0)[1]()
        if t['post'] is not None:
            d = t['post']()
            if d is not None:
                deferred.append((i + 10, d))
    for _, d in deferred:
        d()


def stage_c(nc, P, A, const_base, bank, Tb, c):
    MIX, YS, w_out, xs, gpost_d, out_d = c['MIX'], c['YS'], c['w_out'], c['xs'], c['gpost_d'], c['out_d']
    WOB = c['WOB']
    A.off = const_base
    mixT = A.alloc([32, 1024], BF16)
    Tmix = T()
    wsl = [(A.alloc([32, 512], BF16), T()) for _ in range(2)]
    yst = Rot([(A.alloc([512], F32), T()) for _ in range(3)])
    junk = A.alloc([512], BF16)
    Tjunk = T()
    ssy = [A.alloc([8, 8], F32) for _ in range(2)]
    Tssy = [T(), T()]
    rsy = [A.alloc([8], F32) for _ in range(2)]
    rsyt = [A.alloc([8], F32) for _ in range(2)]
    Trsy = [T(), T()]
    gpost = A.alloc([D], F32)
    Tgp = T()
    yrow = [(A.alloc([1024], F32), T()) for _ in range(4)]
    xrow = [(A.alloc([1024], F32), T()) for _ in range(4)]
    print("stage C arena bytes", A.off)
    P.add('sp', lambda e: e.dma_start(out=gpost, in_=gpost_d.partition_broadcast(128)), writes=[Tgp], dma=1, key='gp')
    mmb = Rot([0, 1, 2, 3, 4, 5, 6, 7])
    mix3 = MIX.rearrange("c p t -> p c t")
    Tout = T()
    TYS = {}

    def load_w(g):
        wbuf, Tw = wsl[g % 2]
        ns = g % 8
        P.add('sp', lambda e: e.dma_start(out=wbuf.rearrange("p a b -> p (a b)"), in_=WOB[ns]), writes=[Tw], dma=1, key=('wo', id(Tw)))

    def final_pass(half, tb):
        P.add('dve', lambda e: e.reduce_sum(out=rsyt[half][:, tb:tb + 1], in_=ssy[half][:, tb, :], axis=AX.X), reads=[Tssy[half]], writes=[Trsy[half]])
        P.add('dve', lambda e: e.tensor_scalar(out=rsyt[half][:, tb:tb + 1], in0=rsyt[half][:, tb:tb + 1], scalar1=1.0 / D, scalar2=EPS, op0=ALU.mult, op1=ALU.add),
              reads=[Trsy[half]], writes=[Trsy[half]])
        P.add('act', lambda e: e.activation(out=rsyt[half][:, tb:tb + 1], in_=rsyt[half][:, tb:tb + 1], func=AF.Sqrt), reads=[Trsy[half]], writes=[Trsy[half]])
        P.add('dve', lambda e: e.reciprocal(out=rsy[half][:, tb:tb + 1], in_=rsyt[half][:, tb:tb + 1]), reads=[Trsy[half]], writes=[Trsy[half]])
        r0 = half * 1024 + tb * 128
        jq = r0 // 512
        g0 = 16 + 512 * (2 * jq + 1) + (r0 % 512)
        for q in range(4):
            yr, Tyr = yrow[q]
            xr, Txr = xrow[q]
            P.add('pool', lambda e, yr=yr, q=q: e.dma_start(out=yr, in_=YS[r0:r0 + 128, q * 1024:(q + 1) * 1024]),
                  reads=[TYS[(half, tb, ns)] for ns in range(q * 2, q * 2 + 2)], writes=[Tyr], dma=1, key=('yr', id(Tyr)))
            P.add('act', lambda e, xr=xr, q=q: e.dma_start(out=xr, in_=xs[g0:g0 + 128, q * 1024:(q + 1) * 1024]), writes=[Txr], dma=1, key=('xr', id(Txr)))
        for q in range(4):
            yr, Tyr = yrow[q]
            xr, Txr = xrow[q]
            P.add('dve', lambda e, yr=yr, q=q: e.scalar_tensor_tensor(out=yr, in0=yr, scalar=rsy[half][:, tb:tb + 1], in1=gpost[:, q * 1024:(q + 1) * 1024], op0=ALU.mult, op1=ALU.mult),
                  reads=[Tyr, Trsy[half], Tgp], writes=[Tyr])
            P.add('dve', lambda e, yr=yr, xr=xr: e.tensor_tensor(out=yr, in0=yr, in1=xr, op=ALU.add), reads=[Tyr, Txr], writes=[Tyr])
            P.add('sp', lambda e, yr=yr, q=q: e.dma_start(out=out_d[r0:r0 + 128, q * 1024:(q + 1) * 1024], in_=yr), reads=[Tyr], writes=[Tout], dma=1, key=('yo', id(Tyr)))

    load_w(0)
    for half in range(2):
        def mfn(e, half=half):
            return [e.dma_start(out=mixT[:, 8 * i:8 * i + 8, :], in_=mix3[:, 8 * i:8 * i + 8, half * 1024:(half + 1) * 1024]) for i in range(4)]
        P.add('pool', mfn, writes=[Tmix], dma=4, key='mix')
        P.add('dve', lambda e, half=half: e.memset(ssy[half], 0.0), writes=[Tssy[half]])
        for ns in range(8):
            g = half * 8 + ns
            wbuf, Tw = wsl[g % 2]
            if g + 1 < 16:
                load_w(g + 1)
            for tb in range(8):
                b = mmb.next()
                for ec in range(32):
                    P.add('pe', lambda e, b=b, ec=ec, tb=tb, wbuf=wbuf: e.matmul(bank[b], lhsT=mixT[:, ec, tb * 128:(tb + 1) * 128], rhs=wbuf[:, ec, :], start=(ec == 0), stop=(ec == 31)),
                          reads=[Tmix, Tw], writes=[Tb[b]])
                ys, Tys = yst.next()
                P.add('dve', lambda e, b=b, ys=ys: e.tensor_copy(out=ys, in_=bank[b]), reads=[Tb[b]], writes=[Tys])
                P.add('act', lambda e, ys=ys, tb=tb, ns=ns, half=half: e.activation(out=junk, in_=ys, func=AF.Square, accum_out=ssy[half][:, tb, ns:ns + 1]), reads=[Tys], writes=[Tjunk, Tssy[half]])
                r0 = half * 1024 + tb * 128
                TYS[(half, tb, ns)] = T()
                P.add('sp', lambda e, ys=ys, r0=r0, ns=ns: e.dma_start(out=YS[r0:r0 + 128, ns * 512:(ns + 1) * 512], in_=ys), reads=[Tys], writes=[TYS[(half, tb, ns)]], dma=1, key=('ys', id(Tys)))
            if half == 1:
                final_pass(0, ns)
    for tb in range(8):
        final_pass(1, tb)
    P.barrier()


def _tables(par):
    st = SLOT_ST[par]
    pos = np.zeros(NTOK, np.int64)
    pos[:16] = np.arange(16) - 16
    for s in range(8):
        pos[16 + 512 * s:16 + 512 * (s + 1)] = 512 * st[s] + np.arange(512)
    inv_freq = (1.0 / (10000.0 ** (np.arange(0, 64, 2, dtype=np.float32) / 64.0))).astype(np.float32)
    ang = (pos + 16).astype(np.float32)[:, None] * inv_freq[None, :]
    cos = np.cos(ang).astype(np.float32)
    sin = np.sin(ang).astype(np.float32)
    p = np.arange(128)
    sign = np.where((p % 64) < 32, -1.0, 1.0).astype(np.float32)
    cosk = np.ascontiguousarray(cos[:, p % 32].T)
    sink = np.ascontiguousarray(sin[:, p % 32].T * sign[:, None])
    own = np.concatenate([np.arange(16 + 512 * (2 * j + 1), 16 + 512 * (2 * j + 2)) for j in range(4)])
    cosq = np.ascontiguousarray(cosk[:, own])
    sinq = np.ascontiguousarray(sink[:, own])
    slopes = (2.0 ** (-(np.arange(1, 9, dtype=np.float32)))).astype(np.float32)
    abias = np.zeros((128, 8, 33, 8), np.float32)
    kl = np.arange(128)
    ccon = np.zeros((128, 4, 8), np.float32)
    for hb in range(8):
        for a in range(4):
            ref_off = (256 * (a // 2) + 128) if hb < 2 else 256
            ccon[:, a, hb] = slopes[hb] * (128 * a + 64 - ref_off)
    for j in range(4):
        qst = st[2 * j + 1]
        for hf in range(2):
            g = 2 * j + hf
            for hb in range(8):
                qref = 512 * qst + ((256 * hf + 128) if hb < 2 else 256)
                kp = np.where(kl < 16, kl - 16, -16)
                abias[:, g, 0, hb] = (kp - qref) * slopes[hb]
                for s in range(8):
                    for b in range(4):
                        kt = 1 + 4 * s + b
                        if st[s] > qst:
                            abias[:, g, kt, hb] = NEG
                        else:
                            kp = 512 * st[s] + 128 * b + kl
                            abias[:, g, kt, hb] = np.minimum((kp - qref) * slopes[hb], 40.0)
    mbias = np.zeros((128, 4, 9), np.float32)
    for j in range(4):
        qst = st[2 * j + 1]
        for s in range(8):
            if st[s] > qst:
                mbias[:, j, 1 + s] = NEG
    b2d = np.zeros((128, 9, 128), np.float32)
    k = np.arange(128)[:, None]
    q = np.arange(128)[None, :]
    vis = (k // 64) <= (q // 64)
    for hb in range(8):
        b2d[:, hb, :] = np.where(vis, -slopes[hb] * np.abs(q - k) + slopes[hb] * (q - 64), NEG)
    b2d[:, 8, :] = np.where(vis, 0.0, NEG)
    import ml_dtypes
    scl = np.array([128.0 ** -0.5] * 8 + [192.0 ** -0.5], np.float32)
    b2s = (b2d / scl[None, :, None]).astype(np.float32)
    b2h = b2s.astype(ml_dtypes.bfloat16).astype(np.float32)
    b2l = (b2s - b2h).astype(ml_dtypes.bfloat16).astype(np.float32)
    return dict(cosk=cosk, sink=sink, cosq=cosq, sinq=sinq, abias=abias.reshape(128, -1), mbias=mbias.reshape(128, -1),
                b2d=np.ascontiguousarray(np.concatenate([b2h.reshape(128, -1), b2l.reshape(128, -1)], axis=1)), ccon=ccon.reshape(128, -1))


def _colmajor(v, n):
    return np.ascontiguousarray(np.asarray(v, np.float32).reshape(n, 128).T)


def make_in_maps(x, meta_tokens, norm_pre, w_in, diff_lambda_q1, diff_lambda_k1, diff_lambda_q2, diff_lambda_k2,
                 diff_subln, mla_norm_q, mla_norm_kv, w_uq, w_ukv, w_out, norm_post):
    f = lambda a: np.ascontiguousarray(np.asarray(a, np.float32))
    x = f(x)
    meta = f(meta_tokens)
    shared = dict(
        w_in=f(w_in)[0], w_uq=f(w_uq)[0], w_ukv=f(w_ukv)[0], w_out=f(w_out)[0],
        gpre=_colmajor(norm_pre[0], 32), gcq=_colmajor(mla_norm_q[0], 12), gckv=_colmajor(mla_norm_kv[0], 4),
        gsub=_colmajor(diff_subln[0], 2), gpost=f(norm_post)[0:1],
        lam4=np.ascontiguousarray(np.stack([f(diff_lambda_q1)[0], f(diff_lambda_k1)[0], f(diff_lambda_q2)[0], f(diff_lambda_k2)[0]])),
        ident=np.eye(128, dtype=np.float32),
    )
    tabs = {0: _tables(0), 1: _tables(1)}
    maps = []
    for c in range(8):
        b, par = c // 2, c % 2
        st = SLOT_ST[par]
        xs = np.concatenate([meta] + [x[b, 512 * st[s]:512 * (st[s] + 1)] for s in range(8)], axis=0)
        m = dict(shared)
        m.update(tabs[par])
        m['xs'] = np.ascontiguousarray(xs)
        maps.append(m)
    return maps


_NC_CACHE = {}


def kernel(**inputs):
    in_maps = make_in_maps(**inputs)
    if 'nc' not in _NC_CACHE:
        _NC_CACHE['nc'] = build_nc()
    nc = _NC_CACHE['nc']
    res = run_bass_kernel_spmd(nc, in_maps, core_ids=list(range(8)))
    out = np.empty((4, 4096, 4096), np.float32)
    for c in range(8):
        b, par = c // 2, c % 2
        st = SLOT_ST[par]
        o = res.results[c]["out"]
        for j in range(4):
            s = st[2 * j + 1]
            out[b, 512 * s:512 * (s + 1)] = o[512 * j:512 * (j + 1)]
    return out
```
